# Optimizing a Trainium2 kernel written in Bass

```python
import jax, jax.numpy as jnp
from jax import lax
import numpy as np

D_MODEL = 4096
BATCH = 8
SEQ = 2048
DEPTH = 1
DEC_BATCH = 1
DEC_SEQ = 8192
PAST_LEN = 128

HEAD_DIM = 128
N_HEADS = D_MODEL // HEAD_DIM
N_KV_HEADS = N_HEADS // 4
GQA_GROUP = N_HEADS // N_KV_HEADS
Q_DIM = N_HEADS * HEAD_DIM
KV_DIM = N_KV_HEADS * HEAD_DIM
CONV_DIM = D_MODEL
CONV_WIDTH = 3
D_FF = 11008
GRID_W = 64
Q_BLOCK = 128
ROPE_THETA = 10000.0
AXIAL_DIM = HEAD_DIM // 2
NORM_EPS = 1e-6
IN_SIZES = [Q_DIM, KV_DIM, KV_DIM, CONV_DIM, CONV_DIM, CONV_DIM, D_MODEL, D_MODEL]
IN_COLS = int(sum(IN_SIZES))
IN_SPLITS = [int(v) for v in np.cumsum(IN_SIZES)[:-1]]

kernel_name = 'hybrid_gqa_shortconv_convffn_encoder'


def rmsnorm(x, g):
    xf = x.astype(jnp.float32)
    y = xf * lax.rsqrt(jnp.mean(xf * xf, axis=-1, keepdims=True) + NORM_EPS)
    return (y * g.astype(jnp.float32)).astype(x.dtype)


def dwconv3(x, w, b):
    xp = jnp.pad(x, ((0, 0), (1, 1), (0, 0)))
    return xp[:, :-2] * w[0] + xp[:, 1:-1] * w[1] + xp[:, 2:] * w[2] + b


def axial_angles(seq_len, dtype):
    n_rows = seq_len // GRID_W
    rows = jnp.repeat(jnp.arange(n_rows, dtype=jnp.int32), GRID_W)
    cols = jnp.arange(seq_len, dtype=jnp.int32) - rows * GRID_W
    inv_freq = ROPE_THETA ** (-jnp.arange(0, AXIAL_DIM, 2, dtype=jnp.float32) / AXIAL_DIM)
    ang_r = rows.astype(jnp.float32)[:, None] * inv_freq[None, :]
    ang_c = cols.astype(jnp.float32)[:, None] * inv_freq[None, :]
    f = lambda a: (jnp.cos(a)[:, None, :].astype(dtype), jnp.sin(a)[:, None, :].astype(dtype))
    return f(ang_r), f(ang_c)


def rope_half(x, cos, sin):
    h = x.shape[-1] // 2
    x1, x2 = x[..., :h], x[..., h:]
    return jnp.concatenate([x1 * cos - x2 * sin, x2 * cos + x1 * sin], axis=-1)


def axial_rope(x, rc, cc):
    return jnp.concatenate([rope_half(x[..., :AXIAL_DIM], *rc), rope_half(x[..., AXIAL_DIM:], *cc)], axis=-1)


def block_attention(q, k, v):
    b, s = q.shape[0], q.shape[1]
    nb = s // Q_BLOCK
    qb = q.reshape(b, nb, Q_BLOCK, N_KV_HEADS, GQA_GROUP, HEAD_DIM).transpose(1, 0, 2, 3, 4, 5)
    scale = HEAD_DIM ** -0.5

    def one_block(q_blk):
        sc = jnp.einsum('bqkgd,bskd->bkgqs', q_blk, k).astype(jnp.float32) * scale
        p = jax.nn.softmax(sc, axis=-1).astype(v.dtype)
        return jnp.einsum('bkgqs,bskd->bqkgd', p, v)

    out = lax.map(one_block, qb)
    return out.transpose(1, 0, 2, 3, 4, 5).reshape(b, s, Q_DIM)


def encoder_layer(h, c, w_ada, b_ada, g_mix_pre, w_in, g_q, g_k, conv_w, conv_b, w_o, g_mix_post,
                  g_ffn_pre, w_up, ffn_conv_w, ffn_conv_b, w_down, g_ffn_post):
    b, s, _ = h.shape
    mod = jnp.einsum('bd,de->be', jax.nn.silu(c), w_ada) + b_ada
    sh1, sc1, gt1, sh2, sc2, gt2 = jnp.split(mod[:, None, :], 6, axis=-1)

    u = rmsnorm(h, g_mix_pre) * (1 + sc1) + sh1
    proj = jnp.einsum('bsd,de->bse', u, w_in)
    q, k, v, gb, gc, xin, ga_attn, ga_conv = jnp.split(proj, IN_SPLITS, axis=-1)
    q = rmsnorm(q.reshape(b, s, N_HEADS, HEAD_DIM), g_q)
    k = rmsnorm(k.reshape(b, s, N_KV_HEADS, HEAD_DIM), g_k)
    v = v.reshape(b, s, N_KV_HEADS, HEAD_DIM)
    rc, cc = axial_angles(s, h.dtype)
    q = axial_rope(q, rc, cc)
    k = axial_rope(k, rc, cc)
    attn = block_attention(q, k, v)
    conv = gb * dwconv3(gc * xin, conv_w, conv_b)
    merged = jax.nn.sigmoid(ga_attn) * attn + jax.nn.sigmoid(ga_conv) * conv
    out = jnp.einsum('bsd,de->bse', merged, w_o)
    h = h + gt1 * rmsnorm(out, g_mix_post)

    u2 = rmsnorm(h, g_ffn_pre) * (1 + sc2) + sh2
    z = dwconv3(jnp.einsum('bsd,df->bsf', u2, w_up), ffn_conv_w, ffn_conv_b)
    za, zb = jnp.split(z, 2, axis=-1)
    y = jnp.einsum('bsf,fd->bsd', jax.nn.silu(za) * zb, w_down)
    h = h + gt2 * rmsnorm(y, g_ffn_post)
    return h


def setup_inputs(seed: int = 0) -> dict:
    key = jax.random.key(seed)
    ks = jax.random.split(key, 20)
    nrm = lambda k, shape, s: jax.random.normal(k, shape, jnp.float32) * s
    gain = lambda k, shape: 1.0 + 0.05 * jax.random.normal(k, shape, jnp.float32)
    return {
        'x_prompt': nrm(ks[0], (BATCH, SEQ, D_MODEL), 1.0),
        'x_sample': nrm(ks[1], (DEC_BATCH, DEC_SEQ, D_MODEL), 1.0),
        'c_prompt': nrm(ks[2], (BATCH, D_MODEL), 1.0),
        'c_sample': nrm(ks[3], (DEC_BATCH, D_MODEL), 1.0),
        'w_ada': nrm(ks[4], (DEPTH, D_MODEL, 6 * D_MODEL), D_MODEL ** -0.5),
        'b_ada': nrm(ks[5], (DEPTH, 6 * D_MODEL), 0.02),
        'g_mix_pre': gain(ks[6], (DEPTH, D_MODEL)),
        'w_in': nrm(ks[7], (DEPTH, D_MODEL, IN_COLS), D_MODEL ** -0.5),
        'g_q': gain(ks[8], (DEPTH, HEAD_DIM)),
        'g_k': gain(ks[9], (DEPTH, HEAD_DIM)),
        'conv_w': nrm(ks[10], (DEPTH, CONV_WIDTH, CONV_DIM), CONV_WIDTH ** -0.5),
        'conv_b': nrm(ks[11], (DEPTH, CONV_DIM), 0.02),
        'w_o': nrm(ks[12], (DEPTH, D_MODEL, D_MODEL), D_MODEL ** -0.5),
        'g_mix_post': gain(ks[13], (DEPTH, D_MODEL)),
        'g_ffn_pre': gain(ks[14], (DEPTH, D_MODEL)),
        'w_up': nrm(ks[15], (DEPTH, D_MODEL, 2 * D_FF), D_MODEL ** -0.5),
        'ffn_conv_w': nrm(ks[16], (DEPTH, CONV_WIDTH, 2 * D_FF), CONV_WIDTH ** -0.5),
        'ffn_conv_b': nrm(ks[17], (DEPTH, 2 * D_FF), 0.02),
        'w_down': nrm(ks[18], (DEPTH, D_FF, D_MODEL), D_FF ** -0.5),
        'g_ffn_post': gain(ks[19], (DEPTH, D_MODEL)),
    }


def reference(x_prompt, x_sample, c_prompt, c_sample, w_ada, b_ada, g_mix_pre, w_in, g_q, g_k,
              conv_w, conv_b, w_o, g_mix_post, g_ffn_pre, w_up, ffn_conv_w, ffn_conv_b, w_down,
              g_ffn_post):
    hp = x_prompt
    hs = x_sample
    for l in range(DEPTH):
        p = (w_ada[l], b_ada[l], g_mix_pre[l], w_in[l], g_q[l], g_k[l], conv_w[l], conv_b[l], w_o[l],
             g_mix_post[l], g_ffn_pre[l], w_up[l], ffn_conv_w[l], ffn_conv_b[l], w_down[l], g_ffn_post[l])
        hp = encoder_layer(hp, c_prompt, *p)
        hs = encoder_layer(hs, c_sample, *p)
    y_prompt = hp
    y_sample = hs
    return (y_prompt, y_sample)
```

```python
import math
import numpy as np
import concourse.bass as bass
import concourse.mybir as mybir
from concourse.bass_utils import run_bass_kernel_spmd

F32 = mybir.dt.float32
BF16 = mybir.dt.bfloat16
AF = mybir.ActivationFunctionType
ALU = mybir.AluOpType
AX = mybir.AxisListType

NORM_EPS = 1e-6
ROPE_THETA = 10000.0
WS = 512


class Cfg:
    def __init__(self, D=4096, DFF=11008, SP=2048, SS=8192, NCORES=8, GRID_W=64,
                 MAX_OUT=508, KVW=512, KB=2048, WSLOTS=3):
        self.D, self.DFF, self.SP, self.SS, self.NC, self.GRID_W = D, DFF, SP, SS, NCORES, GRID_W
        self.KC = D // 128
        self.NH = D // 128
        self.NKV = self.NH // 4
        self.FC = DFF // 128
        self.SH = SS // NCORES
        self.KVW = KVW
        self.KB = KB
        self.WSLOTS = WSLOTS
        self.DBG_TI = -1
        KC = self.KC
        self.NIN = self.NH + 2 * self.NKV + 5 * KC
        self.cK = self.NH
        self.cV = self.NH + self.NKV
        self.cGB = self.NH + 2 * self.NKV
        self.cGC = self.cGB + KC
        self.cXI = self.cGB + 2 * KC
        self.cGA = self.cGB + 3 * KC
        self.cGCV = self.cGB + 4 * KC
        self.main_tiles = []
        for seq, n in ((0, SP), (1, self.SH)):
            nt = -(-n // MAX_OUT)
            base, rem = divmod(n, nt)
            o = 0
            for i in range(nt):
                no = base + (1 if i < rem else 0)
                self.main_tiles.append((seq, o, no))
                o += no
        self.NT = len(self.main_tiles)
        self.kv_tiles = [(0, t, KVW) for t in range(0, SP, KVW)] + [(1, t, KVW) for t in range(0, SS, KVW)]
        off = {}
        o = 0
        for name, n in (("gqbc", 128), ("gkbc", 128), ("gq", 1), ("gk", 1), ("gpre1", KC), ("gpost1", KC),
                        ("gpre2", KC), ("gpost2", KC), ("cw", 3 * KC), ("cb", KC), ("fw", 6 * self.FC),
                        ("fb", 2 * self.FC), ("bada", 6 * KC), ("cT", 2 * KC), ("masks", 2 * self.NT)):
            off[name] = o
            o += n
        self.voff = off
        self.NV = o


class Sched:
    def __init__(self, nc, n_sp=28, n_pool=10):
        self.nc = nc
        self.eng = {"pe": nc.tensor, "act": nc.scalar, "dve": nc.vector, "pool": nc.gpsimd, "sp": nc.sync}
        self.semh = {}
        for e in ("pe", "act", "dve", "pool"):
            self.semh[e] = nc.alloc_semaphore(name="sem_" + e)
        self.ndma = {"sp": n_sp, "pool": n_pool}
        for q, n in self.ndma.items():
            for i in range(n):
                self.semh[("d", q, i)] = nc.alloc_semaphore(name="dsem_%s_%d" % (q, i))
        self.dry = True
        self.reset()

    def reset(self):
        self.cnt = {e: 0 for e in ("pe", "act", "dve", "pool")}
        self.known = {e: {} for e in self.eng}
        self.cells = {}
        self.dexp = {q: [0] * n for q, n in self.ndma.items()}
        self.drr = {q: 0 for q in self.ndma}
        self.ninst = 0

    def _deps(self, e, reads, writes):
        need = {}

        def add(dep, kind):
            if dep is None:
                return
            key, val, peng = dep
            if peng == e:
                if e == "pe" or kind != "raw":
                    return
            if need.get(key, 0) < val:
                need[key] = val

        for c in reads:
            st = self.cells.get(c)
            if st is not None:
                add(st[0], "raw")
        for c in writes:
            st = self.cells.get(c)
            if st is not None:
                add(st[0], "waw")
                for d in st[1].values():
                    add(d, "war")
        kn = self.known[e]
        eng = self.eng[e]
        for key, val in need.items():
            if kn.get(key, 0) < val:
                eng.wait_ge(self.semh[key], val)
                kn[key] = val
                self.ninst += 1

    def _mark(self, me, key, reads, writes):
        for c in reads:
            st = self.cells.get(c)
            if st is None:
                st = [None, {}]
                self.cells[c] = st
            st[1][key] = me
        for c in writes:
            self.cells[c] = [me, {}]

    def op(self, e, fn, reads=(), writes=(), signal=True):
        if self.dry:
            return
        self._deps(e, reads, writes)
        ins = fn(self.eng[e])
        self.ninst += 1
        if signal:
            self.cnt[e] += 1
            ins.then_inc(self.semh[e], 1)
            seq = self.cnt[e]
        else:
            seq = self.cnt[e] + 1
        self._mark((e, seq, e), e, reads, writes)

    def dma(self, q, out, in_, reads=(), writes=()):
        if self.dry:
            return
        self._deps(q, reads, writes)
        idx = self.drr[q]
        self.drr[q] = (idx + 1) % self.ndma[q]
        key = ("d", q, idx)
        prev = self.dexp[q][idx]
        kn = self.known[q]
        if prev and kn.get(key, 0) < prev:
            self.eng[q].wait_ge(self.semh[key], prev)
            kn[key] = prev
        self.eng[q].dma_start(out=out, in_=in_).then_inc(self.semh[key], 16)
        self.ninst += 1
        self.dexp[q][idx] = prev + 16
        self._mark((key, prev + 16, "dma"), key, reads, writes)

    def finish(self):
        if self.dry:
            return
        sp = self.eng["sp"]
        for q, n in self.ndma.items():
            for i in range(n):
                v = self.dexp[q][i]
                if v:
                    sp.wait_ge(self.semh[("d", q, i)], v)
        for e in ("pe", "act", "dve"):
            if self.cnt[e]:
                sp.wait_ge(self.semh[e], self.cnt[e])


class Stream:
    def __init__(self, S, q, slots, depth):
        self.S, self.q, self.slots, self.depth = S, q, slots, depth
        self.reqs = []
        self.rewind()

    def rewind(self):
        self.i = 0
        self.emitted = 0

    def get(self, parts, epoch=0):
        i = self.i
        self.i += 1
        n = len(self.slots)
        if self.S.dry:
            self.reqs.append((epoch, parts))
            return self.slots[i % n]
        lim = min(len(self.reqs), i + self.depth)
        while self.emitted < lim and self.reqs[self.emitted][0] == self.reqs[i][0]:
            j = self.emitted
            slot, cells = self.slots[j % n]
            for dst_fn, src, scells in self.reqs[j][1]:
                self.S.dma(self.q, dst_fn(slot), src, reads=scells, writes=cells)
            self.emitted += 1
        return self.slots[i % n]


class Ring:
    def __init__(self, items):
        self.items = items
        self.i = 0

    def next(self):
        it = self.items[self.i % len(self.items)]
        self.i += 1
        return it


def build_program(cfg, debug=False):
    nc = bass.Bass("TRN2", target_bir_lowering=False)
    KC, FC, NH, NKV, D = cfg.KC, cfg.FC, cfg.NH, cfg.NKV, cfg.D
    SP, SS, SH = cfg.SP, cfg.SS, cfg.SH

    def din(name, shape, dt=F32):
        return nc.dram_tensor(name, list(shape), dt, kind="ExternalInput").ap()

    xp = din("xp", [D, SP + 4])
    xs = din("xs", [D, SH + 4])
    xf = din("xf", [D, SS])
    win = din("win", [cfg.NIN, 128, KC * 128])
    wo = din("wo", [KC, 128, KC * 128])
    wup = din("wup", [2 * FC, 128, KC * 128])
    wdn = din("wdn", [KC, 128, FC * 128])
    wada = din("wada", [6 * KC, 128, KC * 128])
    vecs_d = din("vecs", [128, cfg.NV])
    ident_d = din("ident", [128, 128])
    perm_d = din("perm", [128, 128])
    ropep = din("ropep", [2, 128, SP + 4])
    ropes = din("ropes", [2, 128, SH + 4])
    ropef = din("ropef", [2, 128, SS])
    yp = nc.dram_tensor("yp", [D, SP], F32, kind="ExternalOutput").ap()
    ys = nc.dram_tensor("ys", [D, SH], F32, kind="ExternalOutput").ap()
    seqlen = (SP, SS)
    skind = "ExternalOutput" if debug else "Internal"
    kT_d = [nc.dram_tensor("kT%d" % s, [NKV, 128, seqlen[s]], BF16, kind=skind).ap() for s in (0, 1)]
    v_d = [nc.dram_tensor("vtok%d" % s, [NKV, seqlen[s], 128], BF16, kind=skind).ap() for s in (0, 1)]
    hsp_d = nc.dram_tensor("hsp", [128, KC * WS], F32).ap()
    xmain = (xp, xs)
    xfull = (xp, xf)
    xfull_off = (2, 0)
    rope_main = (ropep, ropes)
    rope_full = (ropep, ropef)
    yout = (yp, ys)

    S = Sched(nc)

    NCELL = max(3 * KC, FC)
    HA = nc.alloc_sbuf_tensor("HA", [128, NCELL * 256], F32)
    HAf = HA[:, :]
    HAb = HAf.bitcast(BF16)
    H3 = HAf[:, 0:KC * WS].rearrange("p (k w) -> p k w", w=WS)
    A3 = HAb[:, 0:FC * WS].rearrange("p (k w) -> p k w", w=WS)
    bufA3 = HAb[:, 2 * KC * WS:3 * KC * WS].rearrange("p (k w) -> p k w", w=WS)
    bufB = nc.alloc_sbuf_tensor("bufB", [128, KC, WS], BF16)

    def cH(c):
        return [("HA", 2 * c), ("HA", 2 * c + 1)]

    def cA(j):
        return [("HA", j)]

    def cBA(c):
        return [("HA", 2 * KC + c)]

    def cBB(c):
        return [("bufB", c)]

    cH_all = [x for c in range(KC) for x in cH(c)]
    cBA_all = [x for c in range(KC) for x in cBA(c)]
    cBB_all = [x for c in range(KC) for x in cBB(c)]
    cA_all = [x for j in range(FC) for x in cA(j)]

    KB = cfg.KB
    kv_cells_per = -(-(4 * KB) // 1024)
    NKVS = 4
    assert NKVS * kv_cells_per <= 2 * KC
    kv_slots = []
    for s_ in range(NKVS):
        view = HAb[:, s_ * kv_cells_per * 512: s_ * kv_cells_per * 512 + 2 * KB]
        kv_slots.append((view, [("HA", s_ * kv_cells_per + i) for i in range(kv_cells_per)]))

    wslots = []
    for i in range(cfg.WSLOTS):
        t = nc.alloc_sbuf_tensor("wslot%d" % i, [128, 32 * 128], BF16)
        wslots.append((t[:, :], [("w", i)]))
    SW = Stream(S, "pool", wslots, cfg.WSLOTS - 1)
    SKV = Stream(S, "sp", kv_slots, NKVS - 1)

    vec = nc.alloc_sbuf_tensor("vec", [128, cfg.NV], F32)
    V = cfg.voff

    def vcol(name, i=0, n=1):
        return vec[:, V[name] + i: V[name] + i + n]

    identb = nc.alloc_sbuf_tensor("identb", [128, 128], BF16)
    permf = nc.alloc_sbuf_tensor("permf", [128, 128], F32)
    ones_b = nc.alloc_sbuf_tensor("ones_b", [128, 128], BF16)
    twos_b = nc.alloc_sbuf_tensor("twos_b", [128, 128], BF16)
    negC = nc.alloc_sbuf_tensor("negC", [128, 4], F32)
    scT = nc.alloc_sbuf_tensor("scT", [128, KC, 2], BF16)
    modT = nc.alloc_sbuf_tensor("modT", [128, 2, 6 * KC], F32)
    der = nc.alloc_sbuf_tensor("der", [128, 2, 4 * KC], F32)
    cwh = nc.alloc_sbuf_tensor("cwh", [128, 4 * KC], F32)
    fwh = nc.alloc_sbuf_tensor("fwh", [128, 8 * FC], F32)
    rstd = nc.alloc_sbuf_tensor("rstd", [128, WS], F32)
    tab = nc.alloc_sbuf_tensor("tab", [128, 2, WS], F32)
    attn_t = nc.alloc_sbuf_tensor("attn_t", [128, 2, WS], F32)
    q_t = nc.alloc_sbuf_tensor("q_t", [128, 2, WS], BF16)
    vT_t = nc.alloc_sbuf_tensor("vT_t", [128, 2, WS], BF16)
    vtok_t = nc.alloc_sbuf_tensor("vtok_t", [128, 2, WS], BF16)
    NF = 8
    ftmp = nc.alloc_sbuf_tensor("ftmp", [128, NF, WS], F32)
    NSQ = 3
    sqt = nc.alloc_sbuf_tensor("sqt", [128, NSQ, WS], BF16)
    NPT = 3
    ptt = nc.alloc_sbuf_tensor("ptt", [128, NPT, WS], BF16)
    Rf = Ring([(ftmp[:, i, :], [("ftmp", i)]) for i in range(NF)])
    Rsq = Ring([(sqt[:, i, :], [("sqt", i)]) for i in range(NSQ)])
    Rpt = Ring([(ptt[:, i, :], [("ptt", i)]) for i in range(NPT)])
    Rattn = Ring([(attn_t[:, i, :], [("attn", i)]) for i in range(2)])
    Rq = Ring([(q_t[:, i, :], [("q", i)]) for i in range(2)])
    Rvtok = Ring([(vtok_t[:, i, :], [("vtok", i)]) for i in range(2)])
    RvT = Ring([(vT_t[:, i, :], [("vT", i)]) for i in range(2)])

    banks = [nc.alloc_psum_tensor("ps%d" % i, [128, WS], F32) for i in range(7)]
    Rwork = Ring([(banks[i][:, :], [("ps", i)]) for i in range(4)])
    bQ = (banks[4][:, :], [("ps", 4)])
    bO = (banks[5][:, :], [("ps", 5)])
    bL = (banks[6][:, :], [("ps", 6)])
    bank7 = nc.alloc_psum_tensor("ps7", [128, 2 * WS], BF16)
    bMb = bank7[:, :]
    bM = (bMb, [("ps", 7)])

    C_VEC, C_CONST, C_MOD, C_DER, C_RSTD, C_TAB, C_VT = [("vec", 0)], [("const", 0)], [("mod", 0)], [("der", 0)], [("rstd", 0)], [("tab", 0)], [("vT", 0)]

    dbg_t = {}

    def dbg(name, ap, cells, dt=F32):
        if not debug:
            return
        if name not in dbg_t:
            dbg_t[name] = nc.dram_tensor("dbg_" + name, list(ap.shape), dt, kind="ExternalOutput").ap()
        S.dma("sp", dbg_t[name], ap, reads=cells, writes=[("dbg", name)])

    def mm_group(bank, lhs_list, rhs_list, N, reads, start=True, stop=True, f32=False):
        out, ocells = bank
        n = len(lhs_list)
        for k in range(n):
            last = (k == n - 1)
            S.op("pe", lambda e, k=k: e.matmul(out[:, :N], lhs_list[k], rhs_list[k],
                                               start=(start and k == 0), stop=(stop and last)),
                 reads=reads if (k == 0 or last) else (), writes=ocells, signal=last)

    DP = 1024

    def wchunk(src_ap, n):
        parts = []
        nfull = (n // DP) * DP
        if nfull:
            parts.append((lambda slot, nfull=nfull: slot[:, 0:nfull].rearrange("p (a b) -> p a b", b=DP),
                          src_ap[:, 0:nfull].rearrange("p (a b) -> p a b", b=DP), ()))
        if n > nfull:
            parts.append((lambda slot, nfull=nfull, n=n: slot[:, nfull:n], src_ap[:, nfull:n], ()))
        return SW.get(parts)

    def act(fn, reads, writes):
        S.op("act", fn, reads, writes)

    def dve(fn, reads, writes):
        S.op("dve", fn, reads, writes)

    def rstd_from(bank, N, n_feat):
        src, scells = bank
        act(lambda e: e.activation(out=rstd[:, :N], in_=src[:, :N], func=AF.Ln, scale=1.0 / n_feat, bias=eps_ap),
            scells + C_CONST, C_RSTD)
        act(lambda e: e.activation(out=rstd[:, :N], in_=rstd[:, :N], func=AF.Exp, scale=-0.5), C_RSTD, C_RSTD)

    CG = 4

    def load_x(src, c0, N):
        sv = src[:, c0:c0 + N].rearrange("(k p) w -> p k w", p=128)
        for k0 in range(0, KC, CG):
            S.dma("sp", H3[:, k0:k0 + CG, :N], sv[:, k0:k0 + CG, :], reads=(),
                  writes=[x for c in range(k0, k0 + CG) for x in cH(c)])

    def load_tab(src, c0, N):
        S.dma("sp", tab[:, :, :N], src[:, :, c0:c0 + N].rearrange("t p w -> p t w"), reads=(), writes=C_TAB)

    def norm_u(seq, N, gm_off, sh_off, dst3, dst_cells, mcols):
        pend = None
        for c in range(KC):
            sq, sqc = Rsq.next()
            act(lambda e, c=c, sq=sq: e.activation(out=sq[:, :N], in_=H3[:, c, :N], func=AF.Square), cH(c), sqc)
            if pend is not None:
                pend()
            pend = (lambda c=c, sq=sq, sqc=sqc: S.op(
                "pe", lambda e: e.matmul(bL[0][:, :N], ones_b[:, :], sq[:, :N], start=(c == 0), stop=(c == KC - 1)),
                reads=sqc + C_CONST, writes=bL[1], signal=True))
        pend()
        rstd_from(bL, N, D)
        for c in range(KC):
            t, tc = Rf.next()
            dve(lambda e, c=c, t=t: e.tensor_tensor(out=t[:, :N], in0=H3[:, c, :N], in1=rstd[:, :N], op=ALU.mult),
                cH(c) + C_RSTD, tc)
            act(lambda e, c=c, t=t: e.activation(out=dst3[:, c, :N], in_=t[:, :N], func=AF.Identity,
                                                 scale=der[:, seq, gm_off + c: gm_off + c + 1],
                                                 bias=modT[:, seq, sh_off + c: sh_off + c + 1]),
                tc + C_DER + C_MOD, dst_cells(c))
        if mcols is not None:
            mL, mR = mcols
            allc = [x for c in range(KC) for x in dst_cells(c)]
            dve(lambda e: e.tensor_scalar(out=dst3[:, :, 0:2], in0=dst3[:, :, 0:2], scalar1=mL, scalar2=None, op0=ALU.mult),
                allc + C_VEC, allc)
            dve(lambda e: e.tensor_scalar(out=dst3[:, :, N - 2:N], in0=dst3[:, :, N - 2:N], scalar1=mR, scalar2=None, op0=ALU.mult),
                allc + C_VEC, allc)

    def proj(w_src, src3, src_cells_all, N, nk=KC, bank=None):
        slot, wc = wchunk(w_src, nk * 128)
        if bank is None:
            bank = Rwork.next()
        mm_group(bank, [slot[:, k * 128:(k + 1) * 128] for k in range(nk)], [src3[:, k, :N] for k in range(nk)], N,
                 reads=wc + src_cells_all)
        return bank

    def qk_norm_rope(bank, N, gcol, dst, dst_cells):
        src, scells = bank
        sq, sqc = Rsq.next()
        act(lambda e: e.activation(out=sq[:, :N], in_=src[:, :N], func=AF.Square), scells, sqc)
        st = {}

        def part1():
            st["kn"] = Rf.next()
            st["rk"] = Rf.next()
            st["t1"] = Rf.next()
            (kn, knc), (rk, rkc), (t1, t1c) = st["kn"], st["rk"], st["t1"]
            ssb = Rwork.next()
            S.op("pe", lambda e: e.matmul(ssb[0][:, :N], ones_b[:, :], sq[:, :N], start=True, stop=True),
                 reads=sqc + C_CONST, writes=ssb[1])
            act(lambda e: e.activation(out=rk[:, :N], in_=ssb[0][:, :N], func=AF.Ln, scale=1.0 / 128, bias=eps_ap),
                ssb[1] + C_CONST, rkc)
            act(lambda e: e.activation(out=rk[:, :N], in_=rk[:, :N], func=AF.Exp, scale=-0.5), rkc, rkc)
            dve(lambda e: e.scalar_tensor_tensor(out=kn[:, :N], in0=src[:, :N], scalar=gcol, in1=rk[:, :N],
                                                 op0=ALU.mult, op1=ALU.mult), scells + rkc + C_VEC, knc)
            dve(lambda e: e.tensor_tensor(out=t1[:, :N], in0=kn[:, :N], in1=tab[:, 0, :N], op=ALU.mult),
                knc + C_TAB, t1c)

        def part2():
            (kn, knc), (rk, rkc), (t1, t1c) = st["kn"], st["rk"], st["t1"]
            rot = Rwork.next()
            S.op("pe", lambda e: e.matmul(rot[0][:, :N], permf[:, :], kn[:, :N], start=True, stop=True),
                 reads=knc + C_CONST, writes=rot[1])
            dve(lambda e: e.tensor_tensor(out=rk[:, :N], in0=rot[0][:, :N], in1=tab[:, 1, :N], op=ALU.mult),
                rot[1] + C_TAB, rkc)
            dve(lambda e: e.tensor_tensor(out=dst[:, :N], in0=t1[:, :N], in1=rk[:, :N], op=ALU.add),
                t1c + rkc, dst_cells)

        return part1, part2

    eps_ap = negC[:, 1:2]

    def setup():
        for c0 in range(0, cfg.NV, DP):
            c1 = min(cfg.NV, c0 + DP)
            S.dma("sp", vec[:, c0:c1], vecs_d[:, c0:c1], (), C_VEC)
        S.dma("sp", permf[:, :], perm_d[:, :], (), C_CONST)
        S.dma("pool", identb[:, :], ident_d[:, :], (), [("identb", 0)])
        dve(lambda e: e.memset(ones_b[:, :], 1.0), (), C_CONST)
        dve(lambda e: e.memset(twos_b[:, :], 2.0), (), C_CONST)
        dve(lambda e: e.memset(negC[:, 1:2], NORM_EPS), (), C_CONST)
        dve(lambda e: e.tensor_reduce(out=negC[:, 2:3], in_=vcol("gqbc", 0, 128), axis=AX.X, op=ALU.max,
                                      apply_absolute_value=True), C_VEC, C_CONST)
        dve(lambda e: e.tensor_reduce(out=negC[:, 3:4], in_=vcol("gkbc", 0, 128), axis=AX.X, op=ALU.max,
                                      apply_absolute_value=True), C_VEC, C_CONST)
        dve(lambda e: e.scalar_tensor_tensor(out=negC[:, 0:1], in0=negC[:, 2:3], scalar=-math.sqrt(128.0),
                                             in1=negC[:, 3:4], op0=ALU.mult, op1=ALU.mult), C_CONST, C_CONST)
        t, tc = Rf.next()
        cT = vcol("cT", 0, 2 * KC)
        act(lambda e: e.activation(out=t[:, :2 * KC], in_=cT, func=AF.Exp, scale=-1.0), C_VEC, tc)
        act(lambda e: e.activation(out=t[:, :2 * KC], in_=t[:, :2 * KC], func=AF.Ln, bias=1.0), tc, tc)
        act(lambda e: e.activation(out=t[:, :2 * KC], in_=t[:, :2 * KC], func=AF.Exp, scale=-1.0), tc, tc)
        dve(lambda e: e.tensor_tensor(out=scT[:, :, :].rearrange("p k s -> p s k"),
                                      in0=t[:, :2 * KC].rearrange("p (s k) -> p s k", k=KC),
                                      in1=cT.rearrange("p (s k) -> p s k", k=KC), op=ALU.mult), tc + C_VEC, [("scT", 0)])
        for j in range(6 * KC):
            slot, wc = wchunk(wada[j], KC * 128)
            bank = Rwork.next()
            mm_group(bank, [slot[:, k * 128:(k + 1) * 128] for k in range(KC)], [scT[:, k, :] for k in range(KC)], 2,
                     reads=wc + [("scT", 0)])
            act(lambda e, j=j, bank=bank: e.activation(out=modT[:, :, j], in_=bank[0][:, 0:2], func=AF.Identity,
                                                       bias=vcol("bada", j), scale=1.0), bank[1] + C_VEC, C_MOD)
        for s_ in (0, 1):
            dve(lambda e, s_=s_: e.scalar_tensor_tensor(out=der[:, s_, 0:KC], in0=modT[:, s_, KC:2 * KC], scalar=1.0,
                                                        in1=vcol("gpre1", 0, KC), op0=ALU.add, op1=ALU.mult), C_MOD + C_VEC, C_DER)
            dve(lambda e, s_=s_: e.tensor_tensor(out=der[:, s_, KC:2 * KC], in0=modT[:, s_, 2 * KC:3 * KC],
                                                 in1=vcol("gpost1", 0, KC), op=ALU.mult), C_MOD + C_VEC, C_DER)
            dve(lambda e, s_=s_: e.scalar_tensor_tensor(out=der[:, s_, 2 * KC:3 * KC], in0=modT[:, s_, 4 * KC:5 * KC], scalar=1.0,
                                                        in1=vcol("gpre2", 0, KC), op0=ALU.add, op1=ALU.mult), C_MOD + C_VEC, C_DER)
            dve(lambda e, s_=s_: e.tensor_tensor(out=der[:, s_, 3 * KC:4 * KC], in0=modT[:, s_, 5 * KC:6 * KC],
                                                 in1=vcol("gpost2", 0, KC), op=ALU.mult), C_MOD + C_VEC, C_DER)
        dve(lambda e: e.tensor_copy(out=cwh[:, 0:3 * KC], in_=vcol("cw", 0, 3 * KC)), C_VEC, C_DER)
        dve(lambda e: e.tensor_copy(out=cwh[:, 3 * KC:4 * KC], in_=vcol("cb", 0, KC)), C_VEC, C_DER)
        dve(lambda e: e.tensor_copy(out=fwh[:, 0:6 * FC], in_=vcol("fw", 0, 6 * FC)), C_VEC, C_DER)
        dve(lambda e: e.tensor_copy(out=fwh[:, 6 * FC:8 * FC], in_=vcol("fb", 0, 2 * FC)), C_VEC, C_DER)

    def sigmoid_from(bank, N, dst, dcells):
        src, scells = bank
        act(lambda e: e.activation(out=dst[:, :N], in_=src[:, :N], func=AF.Exp, scale=-1.0), scells, dcells)
        act(lambda e: e.activation(out=dst[:, :N], in_=dst[:, :N], func=AF.Ln, bias=1.0), dcells, dcells)
        act(lambda e: e.activation(out=dst[:, :N], in_=dst[:, :N], func=AF.Exp, scale=-1.0), dcells, dcells)

    def conv3(y, ycells, N, w0, w1, w2, b, center_src=None, center_cells=None):
        cv, cvc = Rf.next()
        csrc = y if center_src is None else center_src
        ccells = ycells if center_cells is None else center_cells
        act(lambda e: e.activation(out=cv[:, :N], in_=csrc[:, :N], func=AF.Identity, scale=w1, bias=b), ccells + C_DER, cvc)
        dve(lambda e: e.scalar_tensor_tensor(out=cv[:, 1:N], in0=y[:, 0:N - 1], scalar=w0, in1=cv[:, 1:N],
                                             op0=ALU.mult, op1=ALU.add), ycells + cvc + C_DER, cvc)
        dve(lambda e: e.scalar_tensor_tensor(out=cv[:, 0:N - 1], in0=y[:, 1:N], scalar=w2, in1=cv[:, 0:N - 1],
                                             op0=ALU.mult, op1=ALU.add), ycells + cvc + C_DER, cvc)
        return cv, cvc

    def kv_tile(seq, t0, N):
        load_x(xfull[seq], xfull_off[seq] + t0, N)
        load_tab(rope_full[seq], xfull_off[seq] + t0, N)
        norm_u(seq, N, 0, 0, bufA3, cBA, None)
        if t0 == 0 and seq == 0:
            dbg("kv_rstd", rstd[:, :N], C_RSTD)
            dbg("kv_u", bufA3[:, :, :N], cBA_all, BF16)
            dbg("kv_x", H3[:, :, :N], cH_all)
        defer = []

        def drain_to_p1():
            while defer:
                tag, fn = defer.pop(0)
                fn()
                if tag == "p1":
                    break

        for g in range(NKV):
            bank = proj(win[cfg.cK + g], bufA3, cBA_all, N, bank=(bQ if g % 2 == 0 else bO))
            drain_to_p1()
            kr, krc = Rq.next()
            p1, p2 = qk_norm_rope(bank, N, vcol("gk"), kr, krc)
            defer.append(("p1", p1))
            defer.append(("p2", p2))
            defer.append(("st", lambda g=g, kr=kr, krc=krc: S.dma("sp", kT_d[seq][g, :, t0:t0 + N], kr[:, :N], reads=krc,
                                                                 writes=[("kT", seq, g, t0 // KB)])))
        for g in range(NKV):
            bank = proj(win[cfg.cV + g], bufA3, cBA_all, N)
            vT, vTc = RvT.next()
            act(lambda e, bank=bank, vT=vT: e.activation(out=vT[:, :N], in_=bank[0][:, :N], func=AF.Copy), bank[1], vTc)
            drain_to_p1()

            def vpart(g=g, vT=vT, vTc=vTc):
                nsub = N // 128
                for i in range(nsub):
                    S.op("pe", lambda e, i=i: e.transpose(bMb[:, i * 128:(i + 1) * 128], vT[:, i * 128:(i + 1) * 128], identb[:, :]),
                         reads=vTc + [("identb", 0)], writes=bM[1], signal=(i == nsub - 1))
                vt, vtc = Rvtok.next()
                dve(lambda e: e.tensor_copy(out=vt[:, :N], in_=bMb[:, :N]), bM[1], vtc)
                S.dma("sp", v_d[seq][g, t0:t0 + N, :].rearrange("(i p) d -> p i d", p=128),
                      vt[:, :N].rearrange("p (i d) -> p i d", d=128), reads=vtc, writes=[("vtok", seq, g, t0 // KB)])
            defer.append(("p1", vpart))
        while defer:
            defer.pop(0)[1]()

    def main_tile(ti, seq, o0, n_out):
        N = n_out + 4
        Sq = seqlen[seq]
        nblk = Sq // KB
        nkc = KB // 128
        mL = vcol("masks", 2 * ti)
        mR = vcol("masks", 2 * ti + 1)
        load_x(xmain[seq], o0, N)
        load_tab(rope_main[seq], o0, N)
        norm_u(seq, N, 0, 0, bufA3, cBA, (mL, mR))
        defer = []

        def emit_q(h):
            bank = proj(win[h], bufA3, cBA_all, N, bank=bQ)
            qr, qrc = Rq.next()
            p1, p2 = qk_norm_rope(bank, N, vcol("gq"), qr, qrc)
            defer.append(p1)
            defer.append(p2)
            return qr, qrc

        qcur = emit_q(0)
        while defer:
            defer.pop(0)()
        hold = (nblk + SKV.depth <= NKVS)
        kvheld = {}
        for h in range(NH):
            g = h // 4
            qr, qrc = qcur
            if h + 1 < NH:
                qnext = emit_q(h + 1)
            pendpv = None
            nchunks = nblk * nkc
            for b in range(nblk):
                if hold and h % 4 != 0:
                    slot, kvc = kvheld[b]
                else:
                    slot, kvc = SKV.get([
                        (lambda sl: sl[:, 0:KB], kT_d[seq][g, :, b * KB:(b + 1) * KB], [("kT", seq, g, b)]),
                        (lambda sl: sl[:, KB:2 * KB].rearrange("p (i d) -> p i d", d=128),
                         v_d[seq][g, b * KB:(b + 1) * KB, :].rearrange("(i p) d -> p i d", p=128), [("vtok", seq, g, b)]),
                    ], epoch=ti)
                    kvheld[b] = (slot, kvc)
                for kc in range(nkc):
                    ci = b * nkc + kc
                    sb = Rwork.next()
                    S.op("pe", lambda e, sb=sb, slot=slot, kc=kc: e.matmul(sb[0][:, :N], slot[:, kc * 128:(kc + 1) * 128], qr[:, :N],
                                                                          start=True, stop=True),
                         reads=kvc + qrc, writes=sb[1])
                    pt, ptc = Rpt.next()
                    act(lambda e, sb=sb, pt=pt: e.activation(out=pt[:, :N], in_=sb[0][:, :N], func=AF.Exp,
                                                             scale=1.0 / math.sqrt(128.0), bias=negC[:, 0:1]),
                        sb[1] + C_CONST, ptc)
                    if pendpv is not None:
                        pendpv()

                    def pv(slot=slot, kvc=kvc, kc=kc, pt=pt, ptc=ptc, ci=ci):
                        S.op("pe", lambda e: e.matmul(bO[0][:, :N], slot[:, KB + kc * 128: KB + (kc + 1) * 128], pt[:, :N],
                                                      start=(ci == 0), stop=(ci == nchunks - 1)),
                             reads=kvc + ptc, writes=bO[1], signal=False)
                        S.op("pe", lambda e: e.matmul(bL[0][:, :N], ones_b[:, :], pt[:, :N],
                                                      start=(ci == 0), stop=(ci == nchunks - 1)),
                             reads=ptc + C_CONST, writes=bL[1], signal=True)
                    pendpv = pv
            pendpv()
            at, atc = Rattn.next()
            rl, rlc = Rf.next()
            dve(lambda e, rl=rl: e.reciprocal(out=rl[:, :N], in_=bL[0][:, :N]), bL[1], rlc)
            dve(lambda e, rl=rl, at=at: e.tensor_tensor(out=at[:, :N], in0=bO[0][:, :N], in1=rl[:, :N], op=ALU.mult),
                bO[1] + rlc, atc)
            if ti == cfg.DBG_TI:
                dbg("q%d" % h, qr[:, :N], qrc, BF16)
                dbg("rl%d" % h, rl[:, :N], rlc)
                dbg("at%d" % h, at[:, :N], atc)
            if defer:
                defer.pop(0)()
            c = h
            bank = proj(win[cfg.cGA + c], bufA3, cBA_all, N)
            sgA, sgAc = Rf.next()
            sigmoid_from(bank, N, sgA, sgAc)
            bank = proj(win[cfg.cGCV + c], bufA3, cBA_all, N)
            sgC, sgCc = Rf.next()
            sigmoid_from(bank, N, sgC, sgCc)
            if defer:
                defer.pop(0)()
            bank = proj(win[cfg.cGB + c], bufA3, cBA_all, N)
            dve(lambda e, bank=bank, sgC=sgC: e.tensor_tensor(out=sgC[:, :N], in0=bank[0][:, :N], in1=sgC[:, :N], op=ALU.mult),
                bank[1] + sgCc, sgCc)
            bank = proj(win[cfg.cGC + c], bufA3, cBA_all, N)
            gcS, gcc = Rf.next()
            act(lambda e, bank=bank, gcS=gcS: e.activation(out=gcS[:, :N], in_=bank[0][:, :N], func=AF.Copy), bank[1], gcc)
            bank = proj(win[cfg.cXI + c], bufA3, cBA_all, N)
            dve(lambda e, bank=bank, gcS=gcS: e.tensor_tensor(out=gcS[:, :N], in0=bank[0][:, :N], in1=gcS[:, :N], op=ALU.mult),
                bank[1] + gcc, gcc)
            cv, cvc = conv3(gcS, gcc, N, cwh[:, c:c + 1], cwh[:, KC + c:KC + c + 1], cwh[:, 2 * KC + c:2 * KC + c + 1],
                            cwh[:, 3 * KC + c:3 * KC + c + 1])
            dve(lambda e, cv=cv, sgC=sgC: e.tensor_tensor(out=cv[:, :N], in0=cv[:, :N], in1=sgC[:, :N], op=ALU.mult),
                cvc + sgCc, cvc)
            dve(lambda e, sgA=sgA, at=at: e.tensor_tensor(out=sgA[:, :N], in0=sgA[:, :N], in1=at[:, :N], op=ALU.mult),
                sgAc + atc, sgAc)
            dve(lambda e, cv=cv, sgA=sgA, c=c: e.tensor_tensor(out=bufB[:, c, :N], in0=cv[:, :N], in1=sgA[:, :N], op=ALU.add),
                cvc + sgAc, cBB(c))
            while defer:
                defer.pop(0)()
            if h + 1 < NH:
                qcur = qnext

        if ti == cfg.DBG_TI:
            dbg("merged", bufB[:, :, :N], cBB_all, BF16)
            dbg("u1", bufA3[:, :, :N], cBA_all, BF16)
        load_x(xmain[seq], o0, N)
        pend = None
        for m in range(KC):
            bank = proj(wo[m], bufB, cBB_all, N)
            sq, sqc = Rsq.next()
            act(lambda e, bank=bank, m=m: e.activation(out=bufA3[:, m, :N], in_=bank[0][:, :N], func=AF.Copy), bank[1], cBA(m))
            act(lambda e, bank=bank, sq=sq: e.activation(out=sq[:, :N], in_=bank[0][:, :N], func=AF.Square), bank[1], sqc)
            if pend is not None:
                pend()
            pend = (lambda m=m, sq=sq, sqc=sqc: S.op(
                "pe", lambda e: e.matmul(bL[0][:, :N], ones_b[:, :], sq[:, :N], start=(m == 0), stop=(m == KC - 1)),
                reads=sqc + C_CONST, writes=bL[1]))
        pend()
        rstd_from(bL, N, D)
        for c in range(KC):
            t, tc = Rf.next()
            dve(lambda e, c=c, t=t: e.scalar_tensor_tensor(out=t[:, :N], in0=bufA3[:, c, :N], scalar=der[:, seq, KC + c:KC + c + 1],
                                                           in1=rstd[:, :N], op0=ALU.mult, op1=ALU.mult), cBA(c) + C_RSTD + C_DER, tc)
            dve(lambda e, c=c, t=t: e.tensor_tensor(out=H3[:, c, :N], in0=H3[:, c, :N], in1=t[:, :N], op=ALU.add), cH(c) + tc, cH(c))
        norm_u(seq, N, 2 * KC, 3 * KC, bufB, cBB, (mL, mR))
        if ti == cfg.DBG_TI:
            dbg("hmid", H3[:, :, :N], cH_all)
            dbg("u2", bufB[:, :, :N], cBB_all, BF16)
        hv = hsp_d[:, :].rearrange("p (k w) -> p k w", w=WS)
        for k0 in range(0, KC, CG):
            S.dma("sp", hv[:, k0:k0 + CG, :N], H3[:, k0:k0 + CG, :N], reads=[x for c in range(k0, k0 + CG) for x in cH(c)],
                  writes=[("hsp", k0)])

        for j in range(FC):
            bza = proj(wup[j], bufB, cBB_all, N)
            bzb = proj(wup[FC + j], bufB, cBB_all, N)
            za, zac = conv3(bza[0], bza[1], N, fwh[:, j:j + 1], fwh[:, 2 * FC + j:2 * FC + j + 1], fwh[:, 4 * FC + j:4 * FC + j + 1],
                            fwh[:, 6 * FC + j:6 * FC + j + 1])
            jb = FC + j
            zb, zbc = conv3(bzb[0], bzb[1], N, fwh[:, jb:jb + 1], fwh[:, 2 * FC + jb:2 * FC + jb + 1],
                            fwh[:, 4 * FC + jb:4 * FC + jb + 1], fwh[:, 6 * FC + jb:6 * FC + jb + 1])
            sg, sgc = Rf.next()
            act(lambda e, za=za, sg=sg: e.activation(out=sg[:, :N], in_=za[:, :N], func=AF.Exp, scale=-1.0), zac, sgc)
            act(lambda e, sg=sg: e.activation(out=sg[:, :N], in_=sg[:, :N], func=AF.Ln, bias=1.0), sgc, sgc)
            act(lambda e, sg=sg: e.activation(out=sg[:, :N], in_=sg[:, :N], func=AF.Exp, scale=-1.0), sgc, sgc)
            dve(lambda e, za=za, sg=sg: e.tensor_tensor(out=sg[:, :N], in0=sg[:, :N], in1=za[:, :N], op=ALU.mult), sgc + zac, sgc)
            dve(lambda e, zb=zb, sg=sg, j=j: e.tensor_tensor(out=A3[:, j, :N], in0=sg[:, :N], in1=zb[:, :N], op=ALU.mult),
                sgc + zbc, cA(j))
        pend = None
        npc = -(-FC // 32)
        for m in range(KC):
            bank = Rwork.next()
            for pi in range(npc):
                k0 = pi * 32
                nk = min(32, FC - k0)
                slot, wc = wchunk(wdn[m, :, k0 * 128:(k0 + nk) * 128], nk * 128)
                mm_group(bank, [slot[:, k * 128:(k + 1) * 128] for k in range(nk)], [A3[:, k0 + k, :N] for k in range(nk)], N,
                         reads=wc + cA_all, start=(pi == 0), stop=(pi == npc - 1))
            sq, sqc = Rsq.next()
            act(lambda e, bank=bank, m=m: e.activation(out=bufB[:, m, :N], in_=bank[0][:, :N], func=AF.Copy), bank[1], cBB(m))
            act(lambda e, bank=bank, sq=sq: e.activation(out=sq[:, :N], in_=bank[0][:, :N], func=AF.Square), bank[1], sqc)
            if pend is not None:
                pend()
            pend = (lambda m=m, sq=sq, sqc=sqc: S.op(
                "pe", lambda e: e.matmul(bL[0][:, :N], ones_b[:, :], sq[:, :N], start=(m == 0), stop=(m == KC - 1)),
                reads=sqc + C_CONST, writes=bL[1]))
        pend()
        rstd_from(bL, N, D)
        for k0 in range(0, KC, CG):
            S.dma("sp", H3[:, k0:k0 + CG, :N], hv[:, k0:k0 + CG, :N], reads=[("hsp", k0)],
                  writes=[x for c in range(k0, k0 + CG) for x in cH(c)])
        for c in range(KC):
            t, tc = Rf.next()
            dve(lambda e, c=c, t=t: e.scalar_tensor_tensor(out=t[:, :N], in0=bufB[:, c, :N], scalar=der[:, seq, 3 * KC + c:3 * KC + c + 1],
                                                           in1=rstd[:, :N], op0=ALU.mult, op1=ALU.mult), cBB(c) + C_RSTD + C_DER, tc)
            dve(lambda e, c=c, t=t: e.tensor_tensor(out=H3[:, c, :N], in0=H3[:, c, :N], in1=t[:, :N], op=ALU.add), cH(c) + tc, cH(c))
        yv = yout[seq][:, o0:o0 + n_out].rearrange("(k p) w -> p k w", p=128)
        for k0 in range(0, KC, CG):
            S.dma("sp", yv[:, k0:k0 + CG, :], H3[:, k0:k0 + CG, 2:2 + n_out],
                  reads=[x for c in range(k0, k0 + CG) for x in cH(c)], writes=[("yout", seq, ti, k0)])

    def body():
        setup()
        dbg("modT", modT[:, :, :], C_MOD)
        dbg("der", der[:, :, :], C_DER)
        dbg("negC", negC[:, :], C_CONST)
        dbg("scT", scT[:, :, :], [("scT", 0)], BF16)
        for (seq, t0, N) in cfg.kv_tiles:
            kv_tile(seq, t0, N)
        for ti, (seq, o0, n_out) in enumerate(cfg.main_tiles):
            main_tile(ti, seq, o0, n_out)
        S.finish()

    rings = [Rf, Rsq, Rpt, Rattn, Rq, Rvtok, Rwork, RvT]
    S.dry = True
    body()
    S.dry = False
    S.reset()
    for r in rings:
        r.i = 0
    SW.rewind()
    SKV.rewind()
    body()
    return nc, S


def _chunk_w(w, cfg_kc=None):
    K, Nc = w.shape
    a = w.reshape(K // 128, 128, Nc // 128, 128)
    return np.ascontiguousarray(a.transpose(2, 1, 0, 3)).reshape(Nc // 128, 128, (K // 128) * 128)


def _fm(v):
    return np.ascontiguousarray(v.reshape(-1, 128).T)


def _rope_tables(pos, grid_w):
    pos = np.asarray(pos, dtype=np.int64)
    inv = (ROPE_THETA ** (-np.arange(0, 64, 2, dtype=np.float32) / np.float32(64))).astype(np.float32)
    rows = (pos // grid_w).astype(np.float32)
    cols = (pos % grid_w).astype(np.float32)
    out = np.zeros((2, 128, len(pos)), np.float32)
    for d in range(128):
        p = rows if d < 64 else cols
        ang = (p * inv[d % 32]).astype(np.float32)
        out[0, d] = np.cos(ang)
        sn = np.sin(ang)
        out[1, d] = -sn if (d % 64) < 32 else sn
    return out


def prepare_inputs(cfg, x_prompt, x_sample, c_prompt, c_sample, w_ada, b_ada, g_mix_pre, w_in, g_q, g_k,
                   conv_w, conv_b, w_o, g_mix_post, g_ffn_pre, w_up, ffn_conv_w, ffn_conv_b, w_down, g_ffn_post):
    f = lambda a: np.asarray(a, dtype=np.float32)
    KC, FC, SP, SS, SH = cfg.KC, cfg.FC, cfg.SP, cfg.SS, cfg.SH
    shared = {
        "win": _chunk_w(f(w_in)[0]), "wo": _chunk_w(f(w_o)[0]), "wup": _chunk_w(f(w_up)[0]),
        "wdn": _chunk_w(f(w_down)[0]), "wada": _chunk_w(f(w_ada)[0]),
        "ident": np.eye(128, dtype=np.float32),
    }
    perm = np.zeros((128, 128), np.float32)
    for m in range(128):
        perm[m + 32 if (m % 64) < 32 else m - 32, m] = 1.0
    shared["perm"] = perm
    xsT = np.ascontiguousarray(f(x_sample)[0].T)
    shared["xf"] = xsT
    shared["ropef"] = _rope_tables(np.arange(SS), cfg.GRID_W)
    pp = np.concatenate([[0, 0], np.arange(SP), [0, 0]])
    shared["ropep"] = _rope_tables(pp, cfg.GRID_W)
    base = np.zeros((128, cfg.NV), np.float32)
    V = cfg.voff
    base[:, V["gqbc"]:V["gqbc"] + 128] = f(g_q)[0][None, :]
    base[:, V["gkbc"]:V["gkbc"] + 128] = f(g_k)[0][None, :]
    base[:, V["gq"]] = f(g_q)[0]
    base[:, V["gk"]] = f(g_k)[0]
    for nm, arr in (("gpre1", g_mix_pre), ("gpost1", g_mix_post), ("gpre2", g_ffn_pre), ("gpost2", g_ffn_post), ("cb", conv_b)):
        base[:, V[nm]:V[nm] + KC] = _fm(f(arr)[0])
    for j in range(3):
        base[:, V["cw"] + j * KC:V["cw"] + (j + 1) * KC] = _fm(f(conv_w)[0, j])
        base[:, V["fw"] + j * 2 * FC:V["fw"] + (j + 1) * 2 * FC] = _fm(f(ffn_conv_w)[0, j])
    base[:, V["fb"]:V["fb"] + 2 * FC] = _fm(f(ffn_conv_b)[0])
    base[:, V["bada"]:V["bada"] + 6 * KC] = _fm(f(b_ada)[0])
    base[:, V["cT"] + KC:V["cT"] + 2 * KC] = _fm(f(c_sample)[0])
    in_maps = []
    for core in range(cfg.NC):
        m = dict(shared)
        xpT = np.zeros((cfg.D, SP + 4), np.float32)
        xpT[:, 2:SP + 2] = f(x_prompt)[core].T
        m["xp"] = xpT
        s0 = core * SH
        xsm = np.zeros((cfg.D, SH + 4), np.float32)
        lo, hi = max(0, s0 - 2), min(SS, s0 + SH + 2)
        xsm[:, lo - (s0 - 2):hi - (s0 - 2)] = xsT[:, lo:hi]
        m["xs"] = xsm
        pos = np.clip(np.arange(s0 - 2, s0 + SH + 2), 0, SS - 1)
        m["ropes"] = _rope_tables(pos, cfg.GRID_W)
        vv = base.copy()
        vv[:, V["cT"]:V["cT"] + KC] = _fm(f(c_prompt)[core])
        for ti, (seq, o0, n_out) in enumerate(cfg.main_tiles):
            start = 0 if seq == 0 else s0
            Sq = SP if seq == 0 else SS
            vv[:, V["masks"] + 2 * ti] = 0.0 if (start + o0 - 2) < 0 else 1.0
            vv[:, V["masks"] + 2 * ti + 1] = 0.0 if (start + o0 + n_out) >= Sq else 1.0
        m["vecs"] = vv
        in_maps.append(m)
    return in_maps


def run(cfg, inputs):
    nc, S = build_program(cfg)
    in_maps = prepare_inputs(cfg, **inputs)
    res = run_bass_kernel_spmd(nc, in_maps, core_ids=list(range(cfg.NC)))
    yp = np.stack([np.ascontiguousarray(res.results[c]["yp"].T) for c in range(cfg.NC)], axis=0)
    ys = np.concatenate([res.results[c]["ys"].T for c in range(cfg.NC)], axis=0)[None]
    return np.ascontiguousarray(yp, dtype=np.float32), np.ascontiguousarray(ys, dtype=np.float32)


def kernel(**inputs):
    cfg = Cfg()
    return run(cfg, inputs)
```

```python
import math
import numpy as np
import concourse.bass as bass
import concourse.mybir as mybir
from concourse.bass_utils import run_bass_kernel_spmd

F32 = mybir.dt.float32
BF16 = mybir.dt.bfloat16
AF = mybir.ActivationFunctionType
ALU = mybir.AluOpType
AX = mybir.AxisListType

NORM_EPS = 1e-6
ROPE_THETA = 10000.0
WS = 512


class Cfg:
    def __init__(self, D=4096, DFF=11008, SP=2048, SS=8192, NCORES=8, GRID_W=64,
                 MAX_OUT=508, KVW=512, KB=2048, WSLOTS=3):
        self.D, self.DFF, self.SP, self.SS, self.NC, self.GRID_W = D, DFF, SP, SS, NCORES, GRID_W
        self.KC = D // 128
        self.NH = D // 128
        self.NKV = self.NH // 4
        self.FC = DFF // 128
        self.SH = SS // NCORES
        self.KVW = KVW
        self.KB = KB
        self.WSLOTS = WSLOTS
        self.DBG_TI = -1
        KC = self.KC
        self.NIN = self.NH + 2 * self.NKV + 5 * KC
        self.cK = self.NH
        self.cV = self.NH + self.NKV
        self.cGB = self.NH + 2 * self.NKV
        self.cGC = self.cGB + KC
        self.cXI = self.cGB + 2 * KC
        self.cGA = self.cGB + 3 * KC
        self.cGCV = self.cGB + 4 * KC
        self.main_tiles = []
        for seq, n in ((0, SP), (1, self.SH)):
            nt = -(-n // MAX_OUT)
            base, rem = divmod(n, nt)
            o = 0
            for i in range(nt):
                no = base + (1 if i < rem else 0)
                self.main_tiles.append((seq, o, no))
                o += no
        self.NT = len(self.main_tiles)
        self.kv_tiles = [(0, t, KVW) for t in range(0, SP, KVW)] + [(1, t, KVW) for t in range(0, SS, KVW)]
        off = {}
        o = 0
        for name, n in (("gqbc", 128), ("gkbc", 128), ("gq", 1), ("gk", 1), ("gpre1", KC), ("gpost1", KC),
                        ("gpre2", KC), ("gpost2", KC), ("cw", 3 * KC), ("cb", KC), ("fw", 6 * self.FC),
                        ("fb", 2 * self.FC), ("bada", 6 * KC), ("cT", 2 * KC), ("masks", 2 * self.NT)):
            off[name] = o
            o += n
        self.voff = off
        self.NV = o


class Sched:
    def __init__(self, nc, n_sp=28, n_pool=10):
        self.nc = nc
        self.eng = {"pe": nc.tensor, "act": nc.scalar, "dve": nc.vector, "pool": nc.gpsimd, "sp": nc.sync}
        self.semh = {}
        for e in ("pe", "act", "dve", "pool"):
            self.semh[e] = nc.alloc_semaphore(name="sem_" + e)
        self.ndma = {"sp": n_sp, "pool": n_pool}
        for q, n in self.ndma.items():
            for i in range(n):
                self.semh[("d", q, i)] = nc.alloc_semaphore(name="dsem_%s_%d" % (q, i))
        self.dry = True
        self.reset()

    def reset(self):
        self.cnt = {e: 0 for e in ("pe", "act", "dve", "pool")}
        self.known = {e: {} for e in self.eng}
        self.cells = {}
        self.dexp = {q: [0] * n for q, n in self.ndma.items()}
        self.drr = {q: 0 for q in self.ndma}
        self.ninst = 0

    def _deps(self, e, reads, writes):
        need = {}

        def add(dep, kind):
            if dep is None:
                return
            key, val, peng = dep
            if peng == e:
                if e == "pe" or kind != "raw":
                    return
            if need.get(key, 0) < val:
                need[key] = val

        for c in reads:
            st = self.cells.get(c)
            if st is not None:
                add(st[0], "raw")
        for c in writes:
            st = self.cells.get(c)
            if st is not None:
                add(st[0], "waw")
                for d in st[1].values():
                    add(d, "war")
        kn = self.known[e]
        eng = self.eng[e]
        for key, val in need.items():
            if kn.get(key, 0) < val:
                eng.wait_ge(self.semh[key], val)
                kn[key] = val
                self.ninst += 1

    def _mark(self, me, key, reads, writes):
        for c in reads:
            st = self.cells.get(c)
            if st is None:
                st = [None, {}]
                self.cells[c] = st
            st[1][key] = me
        for c in writes:
            self.cells[c] = [me, {}]

    def op(self, e, fn, reads=(), writes=(), signal=True):
        if self.dry:
            return
        self._deps(e, reads, writes)
        ins = fn(self.eng[e])
        self.ninst += 1
        if signal:
            self.cnt[e] += 1
            ins.then_inc(self.semh[e], 1)
            seq = self.cnt[e]
        else:
            seq = self.cnt[e] + 1
        self._mark((e, seq, e), e, reads, writes)

    def dma(self, q, out, in_, reads=(), writes=()):
        if self.dry:
            return
        self._deps(q, reads, writes)
        idx = self.drr[q]
        self.drr[q] = (idx + 1) % self.ndma[q]
        key = ("d", q, idx)
        prev = self.dexp[q][idx]
        kn = self.known[q]
        if prev and kn.get(key, 0) < prev:
            self.eng[q].wait_ge(self.semh[key], prev)
            kn[key] = prev
        self.eng[q].dma_start(out=out, in_=in_).then_inc(self.semh[key], 16)
        self.ninst += 1
        self.dexp[q][idx] = prev + 16
        self._mark((key, prev + 16, "dma"), key, reads, writes)

    def finish(self):
        if self.dry:
            return
        sp = self.eng["sp"]
        for q, n in self.ndma.items():
            for i in range(n):
                v = self.dexp[q][i]
                if v:
                    sp.wait_ge(self.semh[("d", q, i)], v)
        for e in ("pe", "act", "dve"):
            if self.cnt[e]:
                sp.wait_ge(self.semh[e], self.cnt[e])


class Stream:
    def __init__(self, S, q, slots, depth):
        self.S, self.q, self.slots, self.depth = S, q, slots, depth
        self.reqs = []
        self.rewind()

    def rewind(self):
        self.i = 0
        self.emitted = 0

    def get(self, parts, epoch=0):
        i = self.i
        self.i += 1
        n = len(self.slots)
        if self.S.dry:
            self.reqs.append((epoch, parts))
            return self.slots[i % n]
        lim = min(len(self.reqs), i + self.depth)
        while self.emitted < lim and self.reqs[self.emitted][0] == self.reqs[i][0]:
            j = self.emitted
            slot, cells = self.slots[j % n]
            for dst_fn, src, scells in self.reqs[j][1]:
                self.S.dma(self.q, dst_fn(slot), src, reads=scells, writes=cells)
            self.emitted += 1
        return self.slots[i % n]


class Ring:
    def __init__(self, items):
        self.items = items
        self.i = 0

    def next(self):
        it = self.items[self.i % len(self.items)]
        self.i += 1
        return it


def build_program(cfg, debug=False):
    nc = bass.Bass("TRN2", target_bir_lowering=False)
    KC, FC, NH, NKV, D = cfg.KC, cfg.FC, cfg.NH, cfg.NKV, cfg.D
    SP, SS, SH = cfg.SP, cfg.SS, cfg.SH

    def din(name, shape, dt=F32):
        return nc.dram_tensor(name, list(shape), dt, kind="ExternalInput").ap()

    xp = din("xp", [D, SP + 4])
    xs = din("xs", [D, SH + 4])
    xf = din("xf", [D, SS])
    win = din("win", [cfg.NIN, 128, KC * 128])
    wo = din("wo", [KC, 128, KC * 128])
    wup = din("wup", [2 * FC, 128, KC * 128])
    wdn = din("wdn", [KC, 128, FC * 128])
    wada = din("wada", [6 * KC, 128, KC * 128])
    vecs_d = din("vecs", [128, cfg.NV])
    W = {"win": win, "wo": wo, "wup": wup, "wdn": wdn}
    Wb = {"win": nc.dram_tensor("win_b", [cfg.NIN, 128, KC * 128], BF16).ap(),
          "wo": nc.dram_tensor("wo_b", [KC, 128, KC * 128], BF16).ap(),
          "wup": nc.dram_tensor("wup_b", [2 * FC, 128, KC * 128], BF16).ap(),
          "wdn": nc.dram_tensor("wdn_b", [KC, 128, FC * 128], BF16).ap()}
    wseen = set()
    ident_d = din("ident", [128, 128])
    perm_d = din("perm", [128, 128])
    ropep = din("ropep", [2, 128, SP + 4])
    ropes = din("ropes", [2, 128, SH + 4])
    ropef = din("ropef", [2, 128, SS])
    yp = nc.dram_tensor("yp", [D, SP], F32, kind="ExternalOutput").ap()
    ys = nc.dram_tensor("ys", [D, SH], F32, kind="ExternalOutput").ap()
    seqlen = (SP, SS)
    skind = "ExternalOutput" if debug else "Internal"
    kT_d = [nc.dram_tensor("kT%d" % s, [NKV, 128, seqlen[s]], BF16, kind=skind).ap() for s in (0, 1)]
    v_d = [nc.dram_tensor("vtok%d" % s, [NKV, seqlen[s], 128], BF16, kind=skind).ap() for s in (0, 1)]
    hsp_d = nc.dram_tensor("hsp", [128, KC * WS], F32).ap()
    xmain = (xp, xs)
    xfull = (xp, xf)
    xfull_off = (2, 0)
    rope_main = (ropep, ropes)
    rope_full = (ropep, ropef)
    yout = (yp, ys)

    S = Sched(nc)

    NCELL = max(3 * KC, FC)
    HA = nc.alloc_sbuf_tensor("HA", [128, NCELL * 256], F32)
    HAf = HA[:, :]
    HAb = HAf.bitcast(BF16)
    H3 = HAf[:, 0:KC * WS].rearrange("p (k w) -> p k w", w=WS)
    A3 = HAb[:, 0:FC * WS].rearrange("p (k w) -> p k w", w=WS)
    bufA3 = HAb[:, 2 * KC * WS:3 * KC * WS].rearrange("p (k w) -> p k w", w=WS)
    bufB = nc.alloc_sbuf_tensor("bufB", [128, KC, WS], BF16)

    def cH(c):
        return [("HA", 2 * c), ("HA", 2 * c + 1)]

    def cA(j):
        return [("HA", j)]

    def cBA(c):
        return [("HA", 2 * KC + c)]

    def cBB(c):
        return [("bufB", c)]

    cH_all = [x for c in range(KC) for x in cH(c)]
    cBA_all = [x for c in range(KC) for x in cBA(c)]
    cBB_all = [x for c in range(KC) for x in cBB(c)]
    cA_all = [x for j in range(FC) for x in cA(j)]

    KB = cfg.KB
    kv_cells_per = -(-(4 * KB) // 1024)
    NKVS = 4
    assert NKVS * kv_cells_per <= 2 * KC
    kv_slots = []
    for s_ in range(NKVS):
        view = HAb[:, s_ * kv_cells_per * 512: s_ * kv_cells_per * 512 + 2 * KB]
        kv_slots.append((view, [("HA", s_ * kv_cells_per + i) for i in range(kv_cells_per)]))

    wslots = []
    for i in range(cfg.WSLOTS):
        t = nc.alloc_sbuf_tensor("wslot%d" % i, [128, 32 * 128], BF16)
        wslots.append((t[:, :], [("w", i)]))
    SW = Stream(S, "pool", wslots, cfg.WSLOTS - 1)
    SKV = Stream(S, "sp", kv_slots, NKVS - 1)

    vec = nc.alloc_sbuf_tensor("vec", [128, cfg.NV], F32)
    V = cfg.voff

    def vcol(name, i=0, n=1):
        return vec[:, V[name] + i: V[name] + i + n]

    identb = nc.alloc_sbuf_tensor("identb", [128, 128], BF16)
    permf = nc.alloc_sbuf_tensor("permf", [128, 128], F32)
    ones_b = nc.alloc_sbuf_tensor("ones_b", [128, 128], BF16)
    twos_b = nc.alloc_sbuf_tensor("twos_b", [128, 128], BF16)
    negC = nc.alloc_sbuf_tensor("negC", [128, 4], F32)
    scT = nc.alloc_sbuf_tensor("scT", [128, KC, 2], BF16)
    modT = nc.alloc_sbuf_tensor("modT", [128, 2, 6 * KC], F32)
    der = nc.alloc_sbuf_tensor("der", [128, 2, 4 * KC], F32)
    cwh = nc.alloc_sbuf_tensor("cwh", [128, 4 * KC], F32)
    fwh = nc.alloc_sbuf_tensor("fwh", [128, 8 * FC], F32)
    rstd = nc.alloc_sbuf_tensor("rstd", [128, WS], F32)
    tab = nc.alloc_sbuf_tensor("tab", [128, 2, WS], F32)
    attn_t = nc.alloc_sbuf_tensor("attn_t", [128, 2, WS], F32)
    q_t = nc.alloc_sbuf_tensor("q_t", [128, 2, WS], BF16)
    vT_t = nc.alloc_sbuf_tensor("vT_t", [128, 2, WS], BF16)
    vtok_t = nc.alloc_sbuf_tensor("vtok_t", [128, 2, WS], BF16)
    NF = 8
    ftmp = nc.alloc_sbuf_tensor("ftmp", [128, NF, WS], F32)
    NSQ = 3
    sqt = nc.alloc_sbuf_tensor("sqt", [128, NSQ, WS], BF16)
    NPT = 3
    ptt = nc.alloc_sbuf_tensor("ptt", [128, NPT, WS], BF16)
    Rf = Ring([(ftmp[:, i, :], [("ftmp", i)]) for i in range(NF)])
    Rsq = Ring([(sqt[:, i, :], [("sqt", i)]) for i in range(NSQ)])
    Rpt = Ring([(ptt[:, i, :], [("ptt", i)]) for i in range(NPT)])
    Rattn = Ring([(attn_t[:, i, :], [("attn", i)]) for i in range(2)])
    Rq = Ring([(q_t[:, i, :], [("q", i)]) for i in range(2)])
    Rvtok = Ring([(vtok_t[:, i, :], [("vtok", i)]) for i in range(2)])
    RvT = Ring([(vT_t[:, i, :], [("vT", i)]) for i in range(2)])

    banks = [nc.alloc_psum_tensor("ps%d" % i, [128, WS], F32) for i in range(7)]
    Rwork = Ring([(banks[i][:, :], [("ps", i)]) for i in range(4)])
    bQ = (banks[4][:, :], [("ps", 4)])
    bO = (banks[5][:, :], [("ps", 5)])
    bL = (banks[6][:, :], [("ps", 6)])
    bank7 = nc.alloc_psum_tensor("ps7", [128, 2 * WS], BF16)
    bMb = bank7[:, :]
    bM = (bMb, [("ps", 7)])

    C_VEC, C_CONST, C_MOD, C_DER, C_RSTD, C_TAB, C_VT = [("vec", 0)], [("const", 0)], [("mod", 0)], [("der", 0)], [("rstd", 0)], [("tab", 0)], [("vT", 0)]

    dbg_t = {}

    def dbg(name, ap, cells, dt=F32):
        if not debug:
            return
        if name not in dbg_t:
            dbg_t[name] = nc.dram_tensor("dbg_" + name, list(ap.shape), dt, kind="ExternalOutput").ap()
        S.dma("sp", dbg_t[name], ap, reads=cells, writes=[("dbg", name)])

    def mm_group(bank, lhs_list, rhs_list, N, reads, start=True, stop=True, f32=False):
        out, ocells = bank
        n = len(lhs_list)
        for k in range(n):
            last = (k == n - 1)
            S.op("pe", lambda e, k=k: e.matmul(out[:, :N], lhs_list[k], rhs_list[k],
                                               start=(start and k == 0), stop=(stop and last)),
                 reads=reads if (k == 0 or last) else (), writes=ocells, signal=last)

    DP = 1024

    def wchunk(src_ap, n, scr_ap=None, key=None):
        first = (key is None) or (key not in wseen)
        src = src_ap if first else scr_ap
        rc = () if first else [("wscr",) + key]
        parts = []
        nfull = (n // DP) * DP
        if nfull:
            parts.append((lambda slot, nfull=nfull: slot[:, 0:nfull].rearrange("p (a b) -> p a b", b=DP),
                          src[:, 0:nfull].rearrange("p (a b) -> p a b", b=DP), rc))
        if n > nfull:
            parts.append((lambda slot, nfull=nfull, n=n: slot[:, nfull:n], src[:, nfull:n], rc))
        slot, cells = SW.get(parts)
        if first and key is not None:
            wseen.add(key)
            wc_ = [("wscr",) + key]
            if nfull:
                S.dma("sp", scr_ap[:, 0:nfull].rearrange("p (a b) -> p a b", b=DP),
                      slot[:, 0:nfull].rearrange("p (a b) -> p a b", b=DP), reads=cells, writes=wc_)
            if n > nfull:
                S.dma("sp", scr_ap[:, nfull:n], slot[:, nfull:n], reads=cells, writes=wc_)
        return slot, cells

    def act(fn, reads, writes):
        S.op("act", fn, reads, writes)

    def dve(fn, reads, writes):
        S.op("dve", fn, reads, writes)

    def rstd_from(bank, N, n_feat):
        src, scells = bank
        act(lambda e: e.activation(out=rstd[:, :N], in_=src[:, :N], func=AF.Ln, scale=1.0 / n_feat, bias=eps_ap),
            scells + C_CONST, C_RSTD)
        act(lambda e: e.activation(out=rstd[:, :N], in_=rstd[:, :N], func=AF.Exp, scale=-0.5), C_RSTD, C_RSTD)

    CG = 4

    def load_x(src, c0, N):
        sv = src[:, c0:c0 + N].rearrange("(k p) w -> p k w", p=128)
        for k0 in range(0, KC, CG):
            S.dma("sp", H3[:, k0:k0 + CG, :N], sv[:, k0:k0 + CG, :], reads=(),
                  writes=[x for c in range(k0, k0 + CG) for x in cH(c)])

    def load_tab(src, c0, N):
        S.dma("sp", tab[:, :, :N], src[:, :, c0:c0 + N].rearrange("t p w -> p t w"), reads=(), writes=C_TAB)

    def norm_u(seq, N, gm_off, sh_off, dst3, dst_cells, mcols):
        pend = None
        for c in range(KC):
            sq, sqc = Rsq.next()
            act(lambda e, c=c, sq=sq: e.activation(out=sq[:, :N], in_=H3[:, c, :N], func=AF.Square), cH(c), sqc)
            if pend is not None:
                pend()
            pend = (lambda c=c, sq=sq, sqc=sqc: S.op(
                "pe", lambda e: e.matmul(bL[0][:, :N], ones_b[:, :], sq[:, :N], start=(c == 0), stop=(c == KC - 1)),
                reads=sqc + C_CONST, writes=bL[1], signal=True))
        pend()
        rstd_from(bL, N, D)
        for c in range(KC):
            t, tc = Rf.next()
            dve(lambda e, c=c, t=t: e.tensor_tensor(out=t[:, :N], in0=H3[:, c, :N], in1=rstd[:, :N], op=ALU.mult),
                cH(c) + C_RSTD, tc)
            act(lambda e, c=c, t=t: e.activation(out=dst3[:, c, :N], in_=t[:, :N], func=AF.Identity,
                                                 scale=der[:, seq, gm_off + c: gm_off + c + 1],
                                                 bias=modT[:, seq, sh_off + c: sh_off + c + 1]),
                tc + C_DER + C_MOD, dst_cells(c))
        if mcols is not None:
            mL, mR = mcols
            allc = [x for c in range(KC) for x in dst_cells(c)]
            dve(lambda e: e.tensor_scalar(out=dst3[:, :, 0:2], in0=dst3[:, :, 0:2], scalar1=mL, scalar2=None, op0=ALU.mult),
                allc + C_VEC, allc)
            dve(lambda e: e.tensor_scalar(out=dst3[:, :, N - 2:N], in0=dst3[:, :, N - 2:N], scalar1=mR, scalar2=None, op0=ALU.mult),
                allc + C_VEC, allc)

    def proj(wname, widx, src3, src_cells_all, N, nk=KC, bank=None):
        slot, wc = wchunk(W[wname][widx], nk * 128, Wb[wname][widx], (wname, widx))
        if bank is None:
            bank = Rwork.next()
        mm_group(bank, [slot[:, k * 128:(k + 1) * 128] for k in range(nk)], [src3[:, k, :N] for k in range(nk)], N,
                 reads=wc + src_cells_all)
        return bank

    def qk_norm_rope(bank, N, gcol, dst, dst_cells):
        src, scells = bank
        sq, sqc = Rsq.next()
        act(lambda e: e.activation(out=sq[:, :N], in_=src[:, :N], func=AF.Square), scells, sqc)
        st = {}

        def part1():
            st["kn"] = Rf.next()
            st["rk"] = Rf.next()
            st["t1"] = Rf.next()
            (kn, knc), (rk, rkc), (t1, t1c) = st["kn"], st["rk"], st["t1"]
            ssb = Rwork.next()
            S.op("pe", lambda e: e.matmul(ssb[0][:, :N], ones_b[:, :], sq[:, :N], start=True, stop=True),
                 reads=sqc + C_CONST, writes=ssb[1])
            act(lambda e: e.activation(out=rk[:, :N], in_=ssb[0][:, :N], func=AF.Ln, scale=1.0 / 128, bias=eps_ap),
                ssb[1] + C_CONST, rkc)
            act(lambda e: e.activation(out=rk[:, :N], in_=rk[:, :N], func=AF.Exp, scale=-0.5), rkc, rkc)
            dve(lambda e: e.scalar_tensor_tensor(out=kn[:, :N], in0=src[:, :N], scalar=gcol, in1=rk[:, :N],
                                                 op0=ALU.mult, op1=ALU.mult), scells + rkc + C_VEC, knc)
            dve(lambda e: e.tensor_tensor(out=t1[:, :N], in0=kn[:, :N], in1=tab[:, 0, :N], op=ALU.mult),
                knc + C_TAB, t1c)

        def part2():
            (kn, knc), (rk, rkc), (t1, t1c) = st["kn"], st["rk"], st["t1"]
            rot = Rwork.next()
            S.op("pe", lambda e: e.matmul(rot[0][:, :N], permf[:, :], kn[:, :N], start=True, stop=True),
                 reads=knc + C_CONST, writes=rot[1])
            dve(lambda e: e.tensor_tensor(out=rk[:, :N], in0=rot[0][:, :N], in1=tab[:, 1, :N], op=ALU.mult),
                rot[1] + C_TAB, rkc)
            dve(lambda e: e.tensor_tensor(out=dst[:, :N], in0=t1[:, :N], in1=rk[:, :N], op=ALU.add),
                t1c + rkc, dst_cells)

        return part1, part2

    eps_ap = negC[:, 1:2]

    def setup():
        for c0 in range(0, cfg.NV, DP):
            c1 = min(cfg.NV, c0 + DP)
            S.dma("sp", vec[:, c0:c1], vecs_d[:, c0:c1], (), C_VEC)
        S.dma("sp", permf[:, :], perm_d[:, :], (), C_CONST)
        S.dma("pool", identb[:, :], ident_d[:, :], (), [("identb", 0)])
        dve(lambda e: e.memset(ones_b[:, :], 1.0), (), C_CONST)
        dve(lambda e: e.memset(twos_b[:, :], 2.0), (), C_CONST)
        dve(lambda e: e.memset(negC[:, 1:2], NORM_EPS), (), C_CONST)
        dve(lambda e: e.tensor_reduce(out=negC[:, 2:3], in_=vcol("gqbc", 0, 128), axis=AX.X, op=ALU.max,
                                      apply_absolute_value=True), C_VEC, C_CONST)
        dve(lambda e: e.tensor_reduce(out=negC[:, 3:4], in_=vcol("gkbc", 0, 128), axis=AX.X, op=ALU.max,
                                      apply_absolute_value=True), C_VEC, C_CONST)
        dve(lambda e: e.scalar_tensor_tensor(out=negC[:, 0:1], in0=negC[:, 2:3], scalar=-math.sqrt(128.0),
                                             in1=negC[:, 3:4], op0=ALU.mult, op1=ALU.mult), C_CONST, C_CONST)
        t, tc = Rf.next()
        cT = vcol("cT", 0, 2 * KC)
        act(lambda e: e.activation(out=t[:, :2 * KC], in_=cT, func=AF.Exp, scale=-1.0), C_VEC, tc)
        act(lambda e: e.activation(out=t[:, :2 * KC], in_=t[:, :2 * KC], func=AF.Ln, bias=1.0), tc, tc)
        act(lambda e: e.activation(out=t[:, :2 * KC], in_=t[:, :2 * KC], func=AF.Exp, scale=-1.0), tc, tc)
        dve(lambda e: e.tensor_tensor(out=scT[:, :, :].rearrange("p k s -> p s k"),
                                      in0=t[:, :2 * KC].rearrange("p (s k) -> p s k", k=KC),
                                      in1=cT.rearrange("p (s k) -> p s k", k=KC), op=ALU.mult), tc + C_VEC, [("scT", 0)])
        for j in range(6 * KC):
            slot, wc = wchunk(wada[j], KC * 128)
            bank = Rwork.next()
            mm_group(bank, [slot[:, k * 128:(k + 1) * 128] for k in range(KC)], [scT[:, k, :] for k in range(KC)], 2,
                     reads=wc + [("scT", 0)])
            act(lambda e, j=j, bank=bank: e.activation(out=modT[:, :, j], in_=bank[0][:, 0:2], func=AF.Identity,
                                                       bias=vcol("bada", j), scale=1.0), bank[1] + C_VEC, C_MOD)
        for s_ in (0, 1):
            dve(lambda e, s_=s_: e.scalar_tensor_tensor(out=der[:, s_, 0:KC], in0=modT[:, s_, KC:2 * KC], scalar=1.0,
                                                        in1=vcol("gpre1", 0, KC), op0=ALU.add, op1=ALU.mult), C_MOD + C_VEC, C_DER)
            dve(lambda e, s_=s_: e.tensor_tensor(out=der[:, s_, KC:2 * KC], in0=modT[:, s_, 2 * KC:3 * KC],
                                                 in1=vcol("gpost1", 0, KC), op=ALU.mult), C_MOD + C_VEC, C_DER)
            dve(lambda e, s_=s_: e.scalar_tensor_tensor(out=der[:, s_, 2 * KC:3 * KC], in0=modT[:, s_, 4 * KC:5 * KC], scalar=1.0,
                                                        in1=vcol("gpre2", 0, KC), op0=ALU.add, op1=ALU.mult), C_MOD + C_VEC, C_DER)
            dve(lambda e, s_=s_: e.tensor_tensor(out=der[:, s_, 3 * KC:4 * KC], in0=modT[:, s_, 5 * KC:6 * KC],
                                                 in1=vcol("gpost2", 0, KC), op=ALU.mult), C_MOD + C_VEC, C_DER)
        dve(lambda e: e.tensor_copy(out=cwh[:, 0:3 * KC], in_=vcol("cw", 0, 3 * KC)), C_VEC, C_DER)
        dve(lambda e: e.tensor_copy(out=cwh[:, 3 * KC:4 * KC], in_=vcol("cb", 0, KC)), C_VEC, C_DER)
        dve(lambda e: e.tensor_copy(out=fwh[:, 0:6 * FC], in_=vcol("fw", 0, 6 * FC)), C_VEC, C_DER)
        dve(lambda e: e.tensor_copy(out=fwh[:, 6 * FC:8 * FC], in_=vcol("fb", 0, 2 * FC)), C_VEC, C_DER)

    def sigmoid_from(bank, N, dst, dcells):
        src, scells = bank
        act(lambda e: e.activation(out=dst[:, :N], in_=src[:, :N], func=AF.Exp, scale=-1.0), scells, dcells)
        act(lambda e: e.activation(out=dst[:, :N], in_=dst[:, :N], func=AF.Ln, bias=1.0), dcells, dcells)
        act(lambda e: e.activation(out=dst[:, :N], in_=dst[:, :N], func=AF.Exp, scale=-1.0), dcells, dcells)

    def conv3(y, ycells, N, w0, w1, w2, b, center_src=None, center_cells=None):
        cv, cvc = Rf.next()
        csrc = y if center_src is None else center_src
        ccells = ycells if center_cells is None else center_cells
        act(lambda e: e.activation(out=cv[:, :N], in_=csrc[:, :N], func=AF.Identity, scale=w1, bias=b), ccells + C_DER, cvc)
        dve(lambda e: e.scalar_tensor_tensor(out=cv[:, 1:N], in0=y[:, 0:N - 1], scalar=w0, in1=cv[:, 1:N],
                                             op0=ALU.mult, op1=ALU.add), ycells + cvc + C_DER, cvc)
        dve(lambda e: e.scalar_tensor_tensor(out=cv[:, 0:N - 1], in0=y[:, 1:N], scalar=w2, in1=cv[:, 0:N - 1],
                                             op0=ALU.mult, op1=ALU.add), ycells + cvc + C_DER, cvc)
        return cv, cvc

    def kv_tile(seq, t0, N):
        load_x(xfull[seq], xfull_off[seq] + t0, N)
        load_tab(rope_full[seq], xfull_off[seq] + t0, N)
        norm_u(seq, N, 0, 0, bufA3, cBA, None)
        if t0 == 0 and seq == 0:
            dbg("kv_rstd", rstd[:, :N], C_RSTD)
            dbg("kv_u", bufA3[:, :, :N], cBA_all, BF16)
            dbg("kv_x", H3[:, :, :N], cH_all)
        defer = []

        def drain_to_p1():
            while defer:
                tag, fn = defer.pop(0)
                fn()
                if tag == "p1":
                    break

        for g in range(NKV):
            bank = proj("win", cfg.cK + g, bufA3, cBA_all, N, bank=(bQ if g % 2 == 0 else bO))
            drain_to_p1()
            kr, krc = Rq.next()
            p1, p2 = qk_norm_rope(bank, N, vcol("gk"), kr, krc)
            defer.append(("p1", p1))
            defer.append(("p2", p2))
            defer.append(("st", lambda g=g, kr=kr, krc=krc: S.dma("sp", kT_d[seq][g, :, t0:t0 + N], kr[:, :N], reads=krc,
                                                                 writes=[("kT", seq, g, t0 // KB)])))
        for g in range(NKV):
            bank = proj("win", cfg.cV + g, bufA3, cBA_all, N)
            vT, vTc = RvT.next()
            act(lambda e, bank=bank, vT=vT: e.activation(out=vT[:, :N], in_=bank[0][:, :N], func=AF.Copy), bank[1], vTc)
            drain_to_p1()

            def vpart(g=g, vT=vT, vTc=vTc):
                nsub = N // 128
                for i in range(nsub):
                    S.op("pe", lambda e, i=i: e.transpose(bMb[:, i * 128:(i + 1) * 128], vT[:, i * 128:(i + 1) * 128], identb[:, :]),
                         reads=vTc + [("identb", 0)], writes=bM[1], signal=(i == nsub - 1))
                vt, vtc = Rvtok.next()
                dve(lambda e: e.tensor_copy(out=vt[:, :N], in_=bMb[:, :N]), bM[1], vtc)
                S.dma("sp", v_d[seq][g, t0:t0 + N, :].rearrange("(i p) d -> p i d", p=128),
                      vt[:, :N].rearrange("p (i d) -> p i d", d=128), reads=vtc, writes=[("vtok", seq, g, t0 // KB)])
            defer.append(("p1", vpart))
        while defer:
            defer.pop(0)[1]()

    def main_tile(ti, seq, o0, n_out):
        N = n_out + 4
        Sq = seqlen[seq]
        nblk = Sq // KB
        nkc = KB // 128
        mL = vcol("masks", 2 * ti)
        mR = vcol("masks", 2 * ti + 1)
        load_x(xmain[seq], o0, N)
        load_tab(rope_main[seq], o0, N)
        norm_u(seq, N, 0, 0, bufA3, cBA, (mL, mR))
        defer = []

        def emit_q(h):
            bank = proj("win", h, bufA3, cBA_all, N, bank=bQ)
            qr, qrc = Rq.next()
            p1, p2 = qk_norm_rope(bank, N, vcol("gq"), qr, qrc)
            defer.append(p1)
            defer.append(p2)
            return qr, qrc

        qcur = emit_q(0)
        while defer:
            defer.pop(0)()
        hold = (nblk + SKV.depth <= NKVS)
        kvheld = {}
        for h in range(NH):
            g = h // 4
            qr, qrc = qcur
            if h + 1 < NH:
                qnext = emit_q(h + 1)
            pendpv = None
            nchunks = nblk * nkc
            for b in range(nblk):
                if hold and h % 4 != 0:
                    slot, kvc = kvheld[b]
                else:
                    slot, kvc = SKV.get([
                        (lambda sl: sl[:, 0:KB], kT_d[seq][g, :, b * KB:(b + 1) * KB], [("kT", seq, g, b)]),
                        (lambda sl: sl[:, KB:2 * KB].rearrange("p (i d) -> p i d", d=128),
                         v_d[seq][g, b * KB:(b + 1) * KB, :].rearrange("(i p) d -> p i d", p=128), [("vtok", seq, g, b)]),
                    ], epoch=ti)
                    kvheld[b] = (slot, kvc)
                for kc in range(nkc):
                    ci = b * nkc + kc
                    sb = Rwork.next()
                    S.op("pe", lambda e, sb=sb, slot=slot, kc=kc: e.matmul(sb[0][:, :N], slot[:, kc * 128:(kc + 1) * 128], qr[:, :N],
                                                                          start=True, stop=True),
                         reads=kvc + qrc, writes=sb[1])
                    pt, ptc = Rpt.next()
                    act(lambda e, sb=sb, pt=pt: e.activation(out=pt[:, :N], in_=sb[0][:, :N], func=AF.Exp,
                                                             scale=1.0 / math.sqrt(128.0), bias=negC[:, 0:1]),
                        sb[1] + C_CONST, ptc)
                    if pendpv is not None:
                        pendpv()

                    def pv(slot=slot, kvc=kvc, kc=kc, pt=pt, ptc=ptc, ci=ci):
                        S.op("pe", lambda e: e.matmul(bO[0][:, :N], slot[:, KB + kc * 128: KB + (kc + 1) * 128], pt[:, :N],
                                                      start=(ci == 0), stop=(ci == nchunks - 1)),
                             reads=kvc + ptc, writes=bO[1], signal=False)
                        S.op("pe", lambda e: e.matmul(bL[0][:, :N], ones_b[:, :], pt[:, :N],
                                                      start=(ci == 0), stop=(ci == nchunks - 1)),
                             reads=ptc + C_CONST, writes=bL[1], signal=True)
                    pendpv = pv
            pendpv()
            at, atc = Rattn.next()
            rl, rlc = Rf.next()
            dve(lambda e, rl=rl: e.reciprocal(out=rl[:, :N], in_=bL[0][:, :N]), bL[1], rlc)
            dve(lambda e, rl=rl, at=at: e.tensor_tensor(out=at[:, :N], in0=bO[0][:, :N], in1=rl[:, :N], op=ALU.mult),
                bO[1] + rlc, atc)
            if ti == cfg.DBG_TI:
                dbg("q%d" % h, qr[:, :N], qrc, BF16)
                dbg("rl%d" % h, rl[:, :N], rlc)
                dbg("at%d" % h, at[:, :N], atc)
            if defer:
                defer.pop(0)()
            c = h
            bank = proj("win", cfg.cGA + c, bufA3, cBA_all, N)
            sgA, sgAc = Rf.next()
            sigmoid_from(bank, N, sgA, sgAc)
            bank = proj("win", cfg.cGCV + c, bufA3, cBA_all, N)
            sgC, sgCc = Rf.next()
            sigmoid_from(bank, N, sgC, sgCc)
            if defer:
                defer.pop(0)()
            bank = proj("win", cfg.cGB + c, bufA3, cBA_all, N)
            dve(lambda e, bank=bank, sgC=sgC: e.tensor_tensor(out=sgC[:, :N], in0=bank[0][:, :N], in1=sgC[:, :N], op=ALU.mult),
                bank[1] + sgCc, sgCc)
            bank = proj("win", cfg.cGC + c, bufA3, cBA_all, N)
            gcS, gcc = Rf.next()
            act(lambda e, bank=bank, gcS=gcS: e.activation(out=gcS[:, :N], in_=bank[0][:, :N], func=AF.Copy), bank[1], gcc)
            bank = proj("win", cfg.cXI + c, bufA3, cBA_all, N)
            dve(lambda e, bank=bank, gcS=gcS: e.tensor_tensor(out=gcS[:, :N], in0=bank[0][:, :N], in1=gcS[:, :N], op=ALU.mult),
                bank[1] + gcc, gcc)
            cv, cvc = conv3(gcS, gcc, N, cwh[:, c:c + 1], cwh[:, KC + c:KC + c + 1], cwh[:, 2 * KC + c:2 * KC + c + 1],
                            cwh[:, 3 * KC + c:3 * KC + c + 1])
            dve(lambda e, cv=cv, sgC=sgC: e.tensor_tensor(out=cv[:, :N], in0=cv[:, :N], in1=sgC[:, :N], op=ALU.mult),
                cvc + sgCc, cvc)
            dve(lambda e, sgA=sgA, at=at: e.tensor_tensor(out=sgA[:, :N], in0=sgA[:, :N], in1=at[:, :N], op=ALU.mult),
                sgAc + atc, sgAc)
            dve(lambda e, cv=cv, sgA=sgA, c=c: e.tensor_tensor(out=bufB[:, c, :N], in0=cv[:, :N], in1=sgA[:, :N], op=ALU.add),
                cvc + sgAc, cBB(c))
            while defer:
                defer.pop(0)()
            if h + 1 < NH:
                qcur = qnext

        if ti == cfg.DBG_TI:
            dbg("merged", bufB[:, :, :N], cBB_all, BF16)
            dbg("u1", bufA3[:, :, :N], cBA_all, BF16)
        load_x(xmain[seq], o0, N)
        pend = None
        for m in range(KC):
            bank = proj("wo", m, bufB, cBB_all, N)
            sq, sqc = Rsq.next()
            act(lambda e, bank=bank, m=m: e.activation(out=bufA3[:, m, :N], in_=bank[0][:, :N], func=AF.Copy), bank[1], cBA(m))
            act(lambda e, bank=bank, sq=sq: e.activation(out=sq[:, :N], in_=bank[0][:, :N], func=AF.Square), bank[1], sqc)
            if pend is not None:
                pend()
            pend = (lambda m=m, sq=sq, sqc=sqc: S.op(
                "pe", lambda e: e.matmul(bL[0][:, :N], ones_b[:, :], sq[:, :N], start=(m == 0), stop=(m == KC - 1)),
                reads=sqc + C_CONST, writes=bL[1]))
        pend()
        rstd_from(bL, N, D)
        for c in range(KC):
            t, tc = Rf.next()
            dve(lambda e, c=c, t=t: e.scalar_tensor_tensor(out=t[:, :N], in0=bufA3[:, c, :N], scalar=der[:, seq, KC + c:KC + c + 1],
                                                           in1=rstd[:, :N], op0=ALU.mult, op1=ALU.mult), cBA(c) + C_RSTD + C_DER, tc)
            dve(lambda e, c=c, t=t: e.tensor_tensor(out=H3[:, c, :N], in0=H3[:, c, :N], in1=t[:, :N], op=ALU.add), cH(c) + tc, cH(c))
        norm_u(seq, N, 2 * KC, 3 * KC, bufB, cBB, (mL, mR))
        if ti == cfg.DBG_TI:
            dbg("hmid", H3[:, :, :N], cH_all)
            dbg("u2", bufB[:, :, :N], cBB_all, BF16)
        hv = hsp_d[:, :].rearrange("p (k w) -> p k w", w=WS)
        for k0 in range(0, KC, CG):
            S.dma("sp", hv[:, k0:k0 + CG, :N], H3[:, k0:k0 + CG, :N], reads=[x for c in range(k0, k0 + CG) for x in cH(c)],
                  writes=[("hsp", k0)])

        for j in range(FC):
            bza = proj("wup", j, bufB, cBB_all, N)
            bzb = proj("wup", FC + j, bufB, cBB_all, N)
            za, zac = conv3(bza[0], bza[1], N, fwh[:, j:j + 1], fwh[:, 2 * FC + j:2 * FC + j + 1], fwh[:, 4 * FC + j:4 * FC + j + 1],
                            fwh[:, 6 * FC + j:6 * FC + j + 1])
            jb = FC + j
            zb, zbc = conv3(bzb[0], bzb[1], N, fwh[:, jb:jb + 1], fwh[:, 2 * FC + jb:2 * FC + jb + 1],
                            fwh[:, 4 * FC + jb:4 * FC + jb + 1], fwh[:, 6 * FC + jb:6 * FC + jb + 1])
            sg, sgc = Rf.next()
            act(lambda e, za=za, sg=sg: e.activation(out=sg[:, :N], in_=za[:, :N], func=AF.Exp, scale=-1.0), zac, sgc)
            act(lambda e, sg=sg: e.activation(out=sg[:, :N], in_=sg[:, :N], func=AF.Ln, bias=1.0), sgc, sgc)
            act(lambda e, sg=sg: e.activation(out=sg[:, :N], in_=sg[:, :N], func=AF.Exp, scale=-1.0), sgc, sgc)
            dve(lambda e, za=za, sg=sg: e.tensor_tensor(out=sg[:, :N], in0=sg[:, :N], in1=za[:, :N], op=ALU.mult), sgc + zac, sgc)
            dve(lambda e, zb=zb, sg=sg, j=j: e.tensor_tensor(out=A3[:, j, :N], in0=sg[:, :N], in1=zb[:, :N], op=ALU.mult),
                sgc + zbc, cA(j))
        pend = None
        npc = -(-FC // 32)
        for m in range(KC):
            bank = Rwork.next()
            for pi in range(npc):
                k0 = pi * 32
                nk = min(32, FC - k0)
                slot, wc = wchunk(wdn[m, :, k0 * 128:(k0 + nk) * 128], nk * 128,
                                  Wb["wdn"][m, :, k0 * 128:(k0 + nk) * 128], ("wdn", m, pi))
                mm_group(bank, [slot[:, k * 128:(k + 1) * 128] for k in range(nk)], [A3[:, k0 + k, :N] for k in range(nk)], N,
                         reads=wc + cA_all, start=(pi == 0), stop=(pi == npc - 1))
            sq, sqc = Rsq.next()
            act(lambda e, bank=bank, m=m: e.activation(out=bufB[:, m, :N], in_=bank[0][:, :N], func=AF.Copy), bank[1], cBB(m))
            act(lambda e, bank=bank, sq=sq: e.activation(out=sq[:, :N], in_=bank[0][:, :N], func=AF.Square), bank[1], sqc)
            if pend is not None:
                pend()
            pend = (lambda m=m, sq=sq, sqc=sqc: S.op(
                "pe", lambda e: e.matmul(bL[0][:, :N], ones_b[:, :], sq[:, :N], start=(m == 0), stop=(m == KC - 1)),
                reads=sqc + C_CONST, writes=bL[1]))
        pend()
        rstd_from(bL, N, D)
        for k0 in range(0, KC, CG):
            S.dma("sp", H3[:, k0:k0 + CG, :N], hv[:, k0:k0 + CG, :N], reads=[("hsp", k0)],
                  writes=[x for c in range(k0, k0 + CG) for x in cH(c)])
        for c in range(KC):
            t, tc = Rf.next()
            dve(lambda e, c=c, t=t: e.scalar_tensor_tensor(out=t[:, :N], in0=bufB[:, c, :N], scalar=der[:, seq, 3 * KC + c:3 * KC + c + 1],
                                                           in1=rstd[:, :N], op0=ALU.mult, op1=ALU.mult), cBB(c) + C_RSTD + C_DER, tc)
            dve(lambda e, c=c, t=t: e.tensor_tensor(out=H3[:, c, :N], in0=H3[:, c, :N], in1=t[:, :N], op=ALU.add), cH(c) + tc, cH(c))
        yv = yout[seq][:, o0:o0 + n_out].rearrange("(k p) w -> p k w", p=128)
        for k0 in range(0, KC, CG):
            S.dma("sp", yv[:, k0:k0 + CG, :], H3[:, k0:k0 + CG, 2:2 + n_out],
                  reads=[x for c in range(k0, k0 + CG) for x in cH(c)], writes=[("yout", seq, ti, k0)])

    def body():
        setup()
        dbg("modT", modT[:, :, :], C_MOD)
        dbg("der", der[:, :, :], C_DER)
        dbg("negC", negC[:, :], C_CONST)
        dbg("scT", scT[:, :, :], [("scT", 0)], BF16)
        for (seq, t0, N) in cfg.kv_tiles:
            kv_tile(seq, t0, N)
        for ti, (seq, o0, n_out) in enumerate(cfg.main_tiles):
            main_tile(ti, seq, o0, n_out)
        S.finish()

    rings = [Rf, Rsq, Rpt, Rattn, Rq, Rvtok, Rwork, RvT]
    S.dry = True
    body()
    S.dry = False
    S.reset()
    for r in rings:
        r.i = 0
    SW.rewind()
    SKV.rewind()
    wseen.clear()
    body()
    return nc, S


def _chunk_w(w, cfg_kc=None):
    K, Nc = w.shape
    a = w.reshape(K // 128, 128, Nc // 128, 128)
    return np.ascontiguousarray(a.transpose(2, 1, 0, 3)).reshape(Nc // 128, 128, (K // 128) * 128)


def _fm(v):
    return np.ascontiguousarray(v.reshape(-1, 128).T)


def _rope_tables(pos, grid_w):
    pos = np.asarray(pos, dtype=np.int64)
    inv = (ROPE_THETA ** (-np.arange(0, 64, 2, dtype=np.float32) / np.float32(64))).astype(np.float32)
    rows = (pos // grid_w).astype(np.float32)
    cols = (pos % grid_w).astype(np.float32)
    out = np.zeros((2, 128, len(pos)), np.float32)
    for d in range(128):
        p = rows if d < 64 else cols
        ang = (p * inv[d % 32]).astype(np.float32)
        out[0, d] = np.cos(ang)
        sn = np.sin(ang)
        out[1, d] = -sn if (d % 64) < 32 else sn
    return out


def prepare_inputs(cfg, x_prompt, x_sample, c_prompt, c_sample, w_ada, b_ada, g_mix_pre, w_in, g_q, g_k,
                   conv_w, conv_b, w_o, g_mix_post, g_ffn_pre, w_up, ffn_conv_w, ffn_conv_b, w_down, g_ffn_post):
    f = lambda a: np.asarray(a, dtype=np.float32)
    KC, FC, SP, SS, SH = cfg.KC, cfg.FC, cfg.SP, cfg.SS, cfg.SH
    shared = {
        "win": _chunk_w(f(w_in)[0]), "wo": _chunk_w(f(w_o)[0]), "wup": _chunk_w(f(w_up)[0]),
        "wdn": _chunk_w(f(w_down)[0]), "wada": _chunk_w(f(w_ada)[0]),
        "ident": np.eye(128, dtype=np.float32),
    }
    perm = np.zeros((128, 128), np.float32)
    for m in range(128):
        perm[m + 32 if (m % 64) < 32 else m - 32, m] = 1.0
    shared["perm"] = perm
    xsT = np.ascontiguousarray(f(x_sample)[0].T)
    shared["xf"] = xsT
    shared["ropef"] = _rope_tables(np.arange(SS), cfg.GRID_W)
    pp = np.concatenate([[0, 0], np.arange(SP), [0, 0]])
    shared["ropep"] = _rope_tables(pp, cfg.GRID_W)
    base = np.zeros((128, cfg.NV), np.float32)
    V = cfg.voff
    base[:, V["gqbc"]:V["gqbc"] + 128] = f(g_q)[0][None, :]
    base[:, V["gkbc"]:V["gkbc"] + 128] = f(g_k)[0][None, :]
    base[:, V["gq"]] = f(g_q)[0]
    base[:, V["gk"]] = f(g_k)[0]
    for nm, arr in (("gpre1", g_mix_pre), ("gpost1", g_mix_post), ("gpre2", g_ffn_pre), ("gpost2", g_ffn_post), ("cb", conv_b)):
        base[:, V[nm]:V[nm] + KC] = _fm(f(arr)[0])
    for j in range(3):
        base[:, V["cw"] + j * KC:V["cw"] + (j + 1) * KC] = _fm(f(conv_w)[0, j])
        base[:, V["fw"] + j * 2 * FC:V["fw"] + (j + 1) * 2 * FC] = _fm(f(ffn_conv_w)[0, j])
    base[:, V["fb"]:V["fb"] + 2 * FC] = _fm(f(ffn_conv_b)[0])
    base[:, V["bada"]:V["bada"] + 6 * KC] = _fm(f(b_ada)[0])
    base[:, V["cT"] + KC:V["cT"] + 2 * KC] = _fm(f(c_sample)[0])
    in_maps = []
    for core in range(cfg.NC):
        m = dict(shared)
        xpT = np.zeros((cfg.D, SP + 4), np.float32)
        xpT[:, 2:SP + 2] = f(x_prompt)[core].T
        m["xp"] = xpT
        s0 = core * SH
        xsm = np.zeros((cfg.D, SH + 4), np.float32)
        lo, hi = max(0, s0 - 2), min(SS, s0 + SH + 2)
        xsm[:, lo - (s0 - 2):hi - (s0 - 2)] = xsT[:, lo:hi]
        m["xs"] = xsm
        pos = np.clip(np.arange(s0 - 2, s0 + SH + 2), 0, SS - 1)
        m["ropes"] = _rope_tables(pos, cfg.GRID_W)
        vv = base.copy()
        vv[:, V["cT"]:V["cT"] + KC] = _fm(f(c_prompt)[core])
        for ti, (seq, o0, n_out) in enumerate(cfg.main_tiles):
            start = 0 if seq == 0 else s0
            Sq = SP if seq == 0 else SS
            vv[:, V["masks"] + 2 * ti] = 0.0 if (start + o0 - 2) < 0 else 1.0
            vv[:, V["masks"] + 2 * ti + 1] = 0.0 if (start + o0 + n_out) >= Sq else 1.0
        m["vecs"] = vv
        in_maps.append(m)
    return in_maps


def run(cfg, inputs):
    nc, S = build_program(cfg)
    in_maps = prepare_inputs(cfg, **inputs)
    res = run_bass_kernel_spmd(nc, in_maps, core_ids=list(range(cfg.NC)))
    yp = np.stack([np.ascontiguousarray(res.results[c]["yp"].T) for c in range(cfg.NC)], axis=0)
    ys = np.concatenate([res.results[c]["ys"].T for c in range(cfg.NC)], axis=0)[None]
    return np.ascontiguousarray(yp, dtype=np.float32), np.ascontiguousarray(ys, dtype=np.float32)


def kernel(**inputs):
    cfg = Cfg()
    return run(cfg, inputs)
```
